# Optimizing a Trainium2 kernel written in Bass

```python
import jax, jax.numpy as jnp
from jax import lax
import numpy as np

D_MODEL = 2048
BATCH = 4
SEQ = 2048
DEPTH = 1

N_META = 16
D_MIX = D_MODEL
D_LRU = D_MIX // 2
D_CONF = D_MIX - D_LRU
LRU_HEADS = 16
LRU_HEAD_DIM = D_LRU // LRU_HEADS
CONF_GROUPS = 16
LRU_CONV_WIDTH = 4
CONF_KERNEL = 31
LRU_C = 8.0
EPS = 1e-6
IN_COLS = 2 * D_LRU + 3 * D_CONF

kernel_name = "hymba_style_rglru_conformer_hybrid"


def rms_norm(x, w):
    xf = x.astype(jnp.float32)
    var = jnp.mean(xf * xf, axis=-1, keepdims=True)
    return (xf * lax.rsqrt(var + EPS) * w.astype(jnp.float32)).astype(x.dtype)


def layer_norm(x, w, b):
    xf = x.astype(jnp.float32)
    mu = jnp.mean(xf, axis=-1, keepdims=True)
    var = jnp.mean(jnp.square(xf - mu), axis=-1, keepdims=True)
    y = (xf - mu) * lax.rsqrt(var + EPS) * w.astype(jnp.float32) + b.astype(jnp.float32)
    return y.astype(x.dtype)


def causal_depthwise_conv(x, w, b):
    k, c = w.shape
    y = lax.conv_general_dilated(
        x, w.reshape(k, 1, c).astype(x.dtype),
        window_strides=(1,), padding=[(k - 1, 0)],
        dimension_numbers=("NWC", "WIO", "NWC"),
        feature_group_count=c)
    return y + b.astype(x.dtype)


def rg_lru(x, w_a, b_a, w_x, b_x, lam):
    bsz, t, _ = x.shape
    xh = x.reshape(bsz, t, LRU_HEADS, LRU_HEAD_DIM)
    r = jax.nn.sigmoid(jnp.einsum("bthi,hij->bthj", xh, w_a).reshape(bsz, t, D_LRU) + b_a)
    i = jax.nn.sigmoid(jnp.einsum("bthi,hij->bthj", xh, w_x).reshape(bsz, t, D_LRU) + b_x)
    log_a = -LRU_C * r.astype(jnp.float32) * jax.nn.softplus(-lam.astype(jnp.float32))
    a = jnp.exp(log_a)
    mult = jnp.sqrt(-jnp.expm1(2.0 * log_a))
    u = mult * (i * x).astype(jnp.float32)

    def combine(left, right):
        a1, b1 = left
        a2, b2 = right
        return a1 * a2, a2 * b1 + b2

    _, h = lax.associative_scan(combine, (a, u), axis=1)
    return h.astype(x.dtype)


def conformer_conv(u, dw_w, dw_b, ln_w, ln_b, pw_w, pw_b):
    v = u[..., :D_CONF] * jax.nn.sigmoid(u[..., D_CONF:])
    v = causal_depthwise_conv(v, dw_w, dw_b)
    v = jax.nn.silu(layer_norm(v, ln_w, ln_b))
    return v @ pw_w + pw_b


def setup_inputs(seed: int = 0) -> dict:
    key = jax.random.key(seed)
    ks = jax.random.split(key, 24)
    f32 = jnp.float32
    n = lambda k, shape, s: jax.random.normal(k, shape, f32) * s
    x = n(ks[0], (BATCH, SEQ, D_MODEL), 1.0)
    meta_tokens = n(ks[1], (N_META, D_MODEL), 1.0)
    pre_norm_w = 1.0 + n(ks[2], (DEPTH, D_MODEL), 0.02)
    post_norm_w = 1.0 + n(ks[3], (DEPTH, D_MODEL), 0.02)
    w_in = n(ks[4], (DEPTH, D_MODEL, IN_COLS), D_MODEL ** -0.5)
    b_in = n(ks[5], (DEPTH, IN_COLS), 0.01)
    lru_conv_w = n(ks[6], (DEPTH, LRU_CONV_WIDTH, D_LRU), LRU_CONV_WIDTH ** -0.5)
    lru_conv_b = n(ks[7], (DEPTH, D_LRU), 0.01)
    w_gate_a = n(ks[8], (DEPTH, LRU_HEADS, LRU_HEAD_DIM, LRU_HEAD_DIM), LRU_HEAD_DIM ** -0.5)
    b_gate_a = n(ks[9], (DEPTH, D_LRU), 0.01)
    w_gate_x = n(ks[10], (DEPTH, LRU_HEADS, LRU_HEAD_DIM, LRU_HEAD_DIM), LRU_HEAD_DIM ** -0.5)
    b_gate_x = n(ks[11], (DEPTH, D_LRU), 0.01)
    a_c = jax.random.uniform(ks[12], (DEPTH, D_LRU), f32, 0.9, 0.999)
    a0 = a_c ** (1.0 / LRU_C)
    lru_lambda = jnp.log(a0) - jnp.log1p(-a0)
    conf_dw_w = n(ks[13], (DEPTH, CONF_KERNEL, D_CONF), CONF_KERNEL ** -0.5)
    conf_dw_b = n(ks[14], (DEPTH, D_CONF), 0.01)
    conf_ln_w = 1.0 + n(ks[15], (DEPTH, D_CONF), 0.02)
    conf_ln_b = n(ks[16], (DEPTH, D_CONF), 0.01)
    conf_pw_w = n(ks[17], (DEPTH, D_CONF, D_CONF), D_CONF ** -0.5)
    conf_pw_b = n(ks[18], (DEPTH, D_CONF), 0.01)
    w_out = n(ks[19], (DEPTH, D_MIX, D_MODEL), D_MIX ** -0.5)
    return {
        "x": x, "meta_tokens": meta_tokens,
        "pre_norm_w": pre_norm_w, "post_norm_w": post_norm_w,
        "w_in": w_in, "b_in": b_in,
        "lru_conv_w": lru_conv_w, "lru_conv_b": lru_conv_b,
        "w_gate_a": w_gate_a, "b_gate_a": b_gate_a,
        "w_gate_x": w_gate_x, "b_gate_x": b_gate_x,
        "lru_lambda": lru_lambda,
        "conf_dw_w": conf_dw_w, "conf_dw_b": conf_dw_b,
        "conf_ln_w": conf_ln_w, "conf_ln_b": conf_ln_b,
        "conf_pw_w": conf_pw_w, "conf_pw_b": conf_pw_b,
        "w_out": w_out,
    }


def reference(x, meta_tokens, pre_norm_w, post_norm_w, w_in, b_in,
              lru_conv_w, lru_conv_b, w_gate_a, b_gate_a, w_gate_x, b_gate_x,
              lru_lambda, conf_dw_w, conf_dw_b, conf_ln_w, conf_ln_b,
              conf_pw_w, conf_pw_b, w_out):
    bsz = x.shape[0]
    meta = jnp.broadcast_to(meta_tokens.astype(x.dtype)[None], (bsz, N_META, D_MODEL))
    h = jnp.concatenate([meta, x], axis=1)
    for l in range(DEPTH):
        hn = rms_norm(h, pre_norm_w[l])
        z = hn @ w_in[l] + b_in[l]
        x_lru = z[..., :D_LRU]
        g_lru = z[..., D_LRU:2 * D_LRU]
        u_conf = z[..., 2 * D_LRU:2 * D_LRU + 2 * D_CONF]
        g_conf = z[..., 2 * D_LRU + 2 * D_CONF:]
        xc = causal_depthwise_conv(x_lru, lru_conv_w[l], lru_conv_b[l])
        y_lru = rg_lru(xc, w_gate_a[l], b_gate_a[l], w_gate_x[l], b_gate_x[l],
                       lru_lambda[l]) * jax.nn.silu(g_lru)
        y_conf = conformer_conv(u_conf, conf_dw_w[l], conf_dw_b[l], conf_ln_w[l],
                                conf_ln_b[l], conf_pw_w[l], conf_pw_b[l]) * jax.nn.silu(g_conf)
        y = jnp.concatenate([y_lru, y_conf], axis=-1) @ w_out[l]
        h = h + rms_norm(y, post_norm_w[l])
    return h[:, N_META:]
```

```python
import numpy as np
import concourse.bass as bass
import concourse.mybir as mybir
from concourse.bass_utils import run_bass_kernel_spmd

F32 = mybir.dt.float32
BF16 = mybir.dt.bfloat16
I32 = mybir.dt.int32
ALU = mybir.AluOpType
AF = mybir.ActivationFunctionType

D = 2048
DL = 1024
DC = 1024
NCH = 8
TP = 1040
TO = 1024
HALO = 32
EPS = 1e-6
SB_BYTES = 206 * 1024
DT_SIZE = {F32: 4, BF16: 2, I32: 4}


class View:
    __slots__ = ("ap", "space", "ivals")

    def __init__(self, ap, space, ivals):
        self.ap = ap
        self.space = space
        self.ivals = ivals


class Buf:
    def __init__(self, prog, space, off, dims, dt):
        self.prog = prog
        self.space = space
        self.off = off
        self.dims = tuple(dims)
        self.dt = dt
        self.esz = DT_SIZE[dt]
        n = 1
        for d in dims:
            n *= d
        self.nbytes = n * self.esz
        assert off % 4 == 0 and self.nbytes % 4 == 0, (off, dims)
        base = prog.arena[space]
        ap = base[:, off // 4:(off + self.nbytes) // 4]
        if dt != F32:
            ap = ap.bitcast(dt)
        if len(dims) == 2:
            ap = ap.rearrange("p (a b) -> p a b", a=dims[0])
        self.full = ap
        if space == "sb":
            assert off + self.nbytes <= SB_BYTES, ("sbuf overflow", off, self.nbytes)
        else:
            assert off + self.nbytes <= 16384

    def _iv(self, s, e):
        s = self.off + s * self.esz
        e = self.off + e * self.esz
        if self.space == "ps":
            s = (s // 2048) * 2048
            e = ((e + 2047) // 2048) * 2048
        return (s, e)

    def v(self, *idx, parts=None):
        dims = self.dims
        rng = []
        for i, d in enumerate(dims):
            x = idx[i] if i < len(idx) else None
            if x is None:
                rng.append((0, d, False))
            elif isinstance(x, int):
                rng.append((x, x + 1, True))
            else:
                rng.append((x[0], x[1], False))
        for (a, b, _), d in zip(rng, dims):
            assert 0 <= a < b <= d, (idx, dims)
        p = slice(None) if parts is None else slice(parts[0], parts[1])
        if len(dims) == 1:
            a, b, _ = rng[0]
            ap = self.full[p, a:b]
            iv = [self._iv(a, b)]
        else:
            (a0, b0, s0), (a1, b1, _) = rng
            if s0:
                ap = self.full[p, a0, a1:b1]
            else:
                ap = self.full[p, a0:b0, a1:b1]
            if a1 == 0 and b1 == dims[1]:
                iv = [self._iv(a0 * dims[1], b0 * dims[1])]
            else:
                iv = [self._iv(k * dims[1] + a1, k * dims[1] + b1) for k in range(a0, b0)]
        return View(ap, self.space, iv)


class Op:
    __slots__ = ("eng", "fn", "waits", "done", "clock", "idx", "is_dma")


class Prog:
    ENGS = ("pe", "act", "dve", "pool", "sp")

    def __init__(self, nc, sb, ps):
        self.nc = nc
        self.arena = {"sb": sb, "ps": ps}
        self.ops = []
        self.streams = {e: [] for e in self.ENGS}
        self.eng_count = {e: 0 for e in self.ENGS}
        self.clock = {e: {} for e in self.ENGS}
        self.dma_count = {}
        self.recs = {"sb": [], "ps": [], "dram": []}
        self.sb_off = 0

    def alloc(self, dims, dt, off=None, space="sb"):
        if off is None:
            off = self.sb_off
            b = Buf(self, space, off, dims, dt)
            self.sb_off = off + ((b.nbytes + 31) // 32) * 32
            return b
        return Buf(self, space, off, dims, dt)

    def _deps(self, reads, writes):
        deps = set()
        for vw in reads:
            recs = self.recs[vw.space]
            for (s, e) in vw.ivals:
                for r in recs:
                    if r[3] and r[0] < e and s < r[1]:
                        deps.add(r[2])
        for vw in writes:
            recs = self.recs[vw.space]
            for (s, e) in vw.ivals:
                for r in recs:
                    if r[0] < e and s < r[1]:
                        deps.add(r[2])
        return deps

    def _record(self, op_id, eng, reads, writes):
        for vw in writes:
            for (s, e) in vw.ivals:
                recs = self.recs[vw.space]
                recs[:] = [r for r in recs if not (s <= r[0] and r[1] <= e)]
                recs.append((s, e, op_id, True, eng))
        for vw in reads:
            for (s, e) in vw.ivals:
                recs = self.recs[vw.space]
                recs[:] = [r for r in recs if not (r[0] == s and r[1] == e and (not r[3]) and r[4] == eng)]
                recs.append((s, e, op_id, False, eng))

    def op(self, eng, fn, reads=(), writes=(), dma_key=None, extra_deps=()):
        reads = [r for r in reads if r is not None]
        writes = [w for w in writes if w is not None]
        writes = writes + [r for r in reads if r.space == "ps"]
        reads = [r for r in reads if r.space != "ps"]
        deps = self._deps(reads, writes)
        deps.update(extra_deps)
        o = Op()
        o.eng = eng
        o.fn = fn
        o.is_dma = dma_key is not None
        clk = self.clock[eng]
        waits = []
        for d in sorted(deps):
            dop = self.ops[d]
            k, v = dop.done
            if clk.get(k, 0) >= v:
                continue
            waits.append((k, v))
            for kk, vv in dop.clock.items():
                if clk.get(kk, 0) < vv:
                    clk[kk] = vv
        best = {}
        for k, v in waits:
            best[k] = max(best.get(k, 0), v)
        o.waits = sorted(best.items())
        if dma_key is not None:
            n = self.dma_count.get(dma_key, 0) + 1
            self.dma_count[dma_key] = n
            o.done = ("dma:" + dma_key, 16 * n)
        else:
            self.eng_count[eng] += 1
            o.done = (eng, self.eng_count[eng])
        o.clock = dict(clk)
        o.clock[o.done[0]] = o.done[1]
        o.idx = len(self.ops)
        self.ops.append(o)
        self.streams[eng].append(o)
        self._record(o.idx, eng, reads, writes)
        return o.idx

    def emit(self, stack):
        nc = self.nc
        sems = {}
        keys = list(self.ENGS[:4]) + sorted("dma:" + k for k in self.dma_count)
        for k in keys:
            sems[k] = stack.enter_context(nc.semaphore("s_" + k.replace(":", "_")))
        block = stack.enter_context(nc.Block())

        def run(engine_obj, stream, final_waits=()):
            for o in stream:
                for (k, v) in o.waits:
                    engine_obj.wait_ge(sems[k], v)
                ins = o.fn(engine_obj)
                ins.then_inc(sems[o.done[0]], 16 if o.is_dma else 1)
            for (k, v) in final_waits:
                engine_obj.wait_ge(sems[k], v)

        final = [("dma:" + k, 16 * n) for k, n in self.dma_count.items()]

        @block.tensor
        def _(e):
            run(e, self.streams["pe"])

        @block.scalar
        def _(e):
            run(e, self.streams["act"])

        @block.vector
        def _(e):
            run(e, self.streams["dve"])

        @block.gpsimd
        def _(e):
            run(e, self.streams["pool"])

        @block.sync
        def _(e):
            run(e, self.streams["sp"], final)


def interleave(*lists):
    lists = [l for l in lists if l]
    out = []
    if not lists:
        return out
    n = max(len(l) for l in lists)
    pos = [0] * len(lists)
    for step in range(1, n + 1):
        for i, l in enumerate(lists):
            tgt = (step * len(l) + n - 1) // n
            while pos[i] < tgt:
                out.append(l[pos[i]])
                pos[i] += 1
    return out


class _Stop(Exception):
    pass


def build_program(dbg=False, stop=None):
    import contextlib
    nc = bass.Bass("TRN2", target_bir_lowering=False)
    dram = {}

    def din(name, shape):
        dram[name] = nc.dram_tensor(name, list(shape), F32, kind="ExternalInput").ap()
        return dram[name]

    xp_d = din("xp", (9 * 128, D))
    xo_d = din("xo", (TO, D))
    flag_d = din("flag", (128, 1))
    win_d = din("w_in_r", (40, 128, 16, 128))
    wout_d = din("w_out", (D, D))
    pw_d = din("pw_w", (DC, DC))
    wga_d = din("wga_bd", (128, NCH, 128))
    wgx_d = din("wgx_bd", (128, NCH, 128))
    npre_d = din("pre_bc", (128, D))
    npost_d = din("post_bc", (128, D))
    vec_d = din("vecs", (128, 512))
    out_d = nc.dram_tensor("out", [TO, D], F32, kind="ExternalOutput").ap()
    dbg_d = {}

    VC = {}
    c = 0
    for nm, w in (("b_in", 40), ("lcb", 8), ("bga", 8), ("bgx", 8), ("lam", 8), ("cdb", 8),
                  ("lnw", 8), ("lnb", 8), ("pwb", 8), ("lcw", 32), ("cdw", 248)):
        VC[nm] = c
        c += w
    VC["cexp"] = c; c += 8
    VC["c2"] = c; c += 8
    VC["ch"] = c; c += 8
    VC["hga"] = c; c += 8
    VC["hgx"] = c; c += 8
    VC["hbu"] = c; c += 8
    VC["qtr"] = c; c += 1
    VC["tmp"] = c; c += 8
    VC["eps"] = c; c += 1
    VC["one"] = c; c += 1
    VC["flag"] = c; c += 1
    VC["hst"] = c; c += 8
    VC["hin"] = c; c += 8
    VC["ssq"] = c; c += 48
    assert c <= 512

    stack = contextlib.ExitStack()
    with stack:
        sb = stack.enter_context(nc.sbuf_tensor("arena", [128, SB_BYTES // 4], F32))
        ps = stack.enter_context(nc.psum_tensor("psum", [128, 4096], F32))
        P = Prog(nc, sb, ps)

        vecs = P.alloc((512,), F32)
        ident = P.alloc((128,), BF16)
        ones = P.alloc((128,), BF16)
        wga = P.alloc((NCH, 128), BF16)
        wgx = P.alloc((NCH, 128), BF16)
        cdwb = P.alloc((256,), BF16)
        iota_i = P.alloc((128,), I32)
        iota_f = P.alloc((128,), F32)
        CB = [P.alloc((512,), BF16) for _ in range(2)]
        C2B = [P.alloc((512,), BF16) for _ in range(2)]
        yl0 = P.sb_off
        YL = P.alloc((NCH, TO), BF16)
        YC = P.alloc((NCH, TO), BF16)
        P.alloc((256,), BF16)
        HNTP = P.alloc((16, TP), BF16, off=yl0)
        assert HNTP.nbytes <= P.sb_off - yl0
        v0 = P.sb_off
        V = P.alloc((NCH, HALO + TO), BF16)
        XLP = P.alloc((NCH, TP + 4), BF16, off=v0)
        assert XLP.nbytes <= V.nbytes
        XLO = P.alloc((NCH, TO + 4), BF16, off=yl0 + 16384)
        assert yl0 + 16384 + XLO.nbytes <= v0
        dg4 = P.alloc((NCH * 4, 128), BF16)
        g0 = P.sb_off
        SG1 = P.alloc((HALO + TO,), F32)
        MU = P.alloc((512,), F32)
        RSTD = P.alloc((512,), F32)
        MSQ = P.alloc((512,), F32)
        NPRE = P.alloc((D,), F32, off=g0)
        DG31X = P.alloc((31, 128), BF16, off=g0)
        assert DG31X.nbytes <= 4224 + 3 * 2048
        junk = CB[0]
        y0 = P.sb_off
        NSLOT = 5
        HNT = P.alloc((16, HALO + TO), BF16, off=y0)
        WIN = [P.alloc((16, 128), BF16, off=y0 + HNT.nbytes + i * 4096) for i in range(NSLOT)]
        ysz = HNT.nbytes + NSLOT * 4096
        C = P.alloc((NCH, 512), F32, off=y0)
        S = P.alloc((NCH, 512), BF16, off=y0 + C.nbytes)
        DG31 = [P.alloc((31, 128), BF16, off=y0 + C.nbytes + S.nbytes + i * 7936) for i in range(2)]
        pwy = y0 + C.nbytes + S.nbytes + 2 * 7936
        assert pwy + 4 * 2048 <= y0 + ysz
        XE = [P.alloc((D,), F32, off=v0)]
        OUT = [P.alloc((D,), F32, off=v0 + 8192)]
        NPOST = P.alloc((D,), F32, off=y0 + C.nbytes + S.nbytes)
        x0 = y0 + ysz
        XS = [P.alloc((D,), F32, off=x0 + i * 8192) for i in range(2)]
        HNB = [P.alloc((D,), BF16, off=x0 + 16384 + i * 4096) for i in range(2)]
        JK = P.alloc((D,), BF16, off=x0 + 24576)
        lo = x0 + 28672
        TL = TP
        Rb = [P.alloc((TL,), F32, off=lo + (0 + i) * 4160) for i in range(2)]
        Ib = [P.alloc((TL,), F32, off=lo + (2 + i) * 4160) for i in range(2)]
        Ab = [P.alloc((TL,), F32, off=lo + (4 + i) * 4160) for i in range(2)]
        Hb = [P.alloc((TL,), F32, off=lo + (6 + i) * 4160) for i in range(2)]
        XC = [P.alloc((TL,), F32, off=lo + (8 + i) * 4160) for i in range(2)]
        XCB = [P.alloc((TL,), BF16, off=lo + 10 * 4160 + i * 2112) for i in range(2)]
        XS += [P.alloc((D,), F32, off=lo + i * 8192) for i in range(2)]
        HNB += [P.alloc((D,), BF16, off=lo + 16384)]
        xsz = 28672 + 10 * 4160 + 2 * 2112
        WOUT = [P.alloc((D,), BF16, off=x0 + i * 4096) for i in range(16)]
        assert xsz >= 16 * 4096 + 4 * 2048
        PWc = [P.alloc((DC,), BF16, off=x0 + 65536 + i * 2048) for i in range(4)] + \
              [P.alloc((DC,), BF16, off=pwy + i * 2048) for i in range(4)]
        assert x0 + xsz <= SB_BYTES, (x0 + xsz, SB_BYTES)
        PSB = [P.alloc((512,), F32, off=b * 2048, space="ps") for b in range(8)]
        PST = P.alloc((16, 128), BF16, off=6 * 2048, space="ps")

        def vcol(nm, j=0, n=1):
            return vecs.v((VC[nm] + j, VC[nm] + j + n))

        def dma(eng, out_v, in_ap, key, reads=()):
            return P.op(eng, lambda e: e.dma_start(out=out_v.ap, in_=in_ap), reads=reads, writes=[out_v], dma_key=key)

        def dump(name, view, shape):
            if not dbg:
                return
            if name not in dbg_d:
                dbg_d[name] = nc.dram_tensor("dbg_" + name, list(shape), view.ap.dtype, kind="ExternalOutput").ap()
            P.op("sp", lambda e: e.dma_start(out=dbg_d[name], in_=view.ap), reads=[view], dma_key="out_dbg_" + name)

        def act(out_v, in_v, func, bias=None, scale=None, accum=None, extra_r=()):
            kw = {}
            rd = [in_v] + list(extra_r)
            if bias is not None:
                if isinstance(bias, View):
                    kw["bias"] = bias.ap
                    rd.append(bias)
                else:
                    kw["bias"] = bias
            if scale is not None:
                if isinstance(scale, View):
                    kw["scale"] = scale.ap
                    rd.append(scale)
                else:
                    kw["scale"] = scale
            wr = [out_v]
            if accum is not None:
                kw["accum_out"] = accum.ap
                wr.append(accum)
            return P.op("act", lambda e: e.activation(out=out_v.ap, in_=in_v.ap, func=func, **kw), reads=rd, writes=wr)

        def tt(eng, out_v, a, b, op):
            return P.op(eng, lambda e: e.tensor_tensor(out=out_v.ap, in0=a.ap, in1=b.ap, op=op), reads=[a, b], writes=[out_v])

        def ts(eng, out_v, a, s1, op0, s2=None, op1=None):
            rd = [a]
            s1a = s1.ap if isinstance(s1, View) else s1
            s2a = s2.ap if isinstance(s2, View) else s2
            if isinstance(s1, View):
                rd.append(s1)
            if isinstance(s2, View):
                rd.append(s2)
            if op1 is None:
                return P.op(eng, lambda e: e.tensor_scalar(out=out_v.ap, in0=a.ap, scalar1=s1a, scalar2=None, op0=op0), reads=rd, writes=[out_v])
            return P.op(eng, lambda e: e.tensor_scalar(out=out_v.ap, in0=a.ap, scalar1=s1a, scalar2=s2a, op0=op0, op1=op1), reads=rd, writes=[out_v])

        def stt(eng, out_v, a, s, b, op0, op1):
            rd = [a, b]
            sa = s.ap if isinstance(s, View) else s
            if isinstance(s, View):
                rd.append(s)
            return P.op(eng, lambda e: e.scalar_tensor_tensor(out=out_v.ap, in0=a.ap, scalar=sa, in1=b.ap, op0=op0, op1=op1), reads=rd, writes=[out_v])

        def mm_group(out_v, pairs, reads):
            def fn(e):
                ins = None
                n = len(pairs)
                for i, (l, r) in enumerate(pairs):
                    ins = e.matmul(out_v.ap, lhsT=l, rhs=r, start=(i == 0), stop=(i == n - 1))
                return ins
            return P.op("pe", fn, reads=reads, writes=[out_v])

        NROW = XS[3].v(parts=(0, 1))
        dma("sp", NROW, npre_d[0:1, :], "c_npre")
        dma("sp", XS[0].v(), xp_d[0:128, :], "xs0")
        dma("sp", vecs.v((0, VC["cexp"])), vec_d[:, 0:VC["cexp"]], "c_vecs")
        dma("sp", vecs.v((VC["flag"], VC["flag"] + 1)), flag_d, "c_flag")
        ONESF = P.alloc((128,), F32, off=JK.off)
        P.op("dve", lambda e: e.memset(ONESF.v().ap, 1.0), writes=[ONESF.v()])
        for b_ in range(4):
            pv_ = PSB[b_].v()
            rhs_ = View(NROW.ap[:, b_ * 512:(b_ + 1) * 512], NROW.space, NROW.ivals)
            P.op("pe", lambda e, pv_=pv_, rhs_=rhs_: e.matmul(pv_.ap, lhsT=ONESF.v(parts=(0, 1)).ap, rhs=rhs_.ap, start=True, stop=True),
                 reads=[ONESF.v(), rhs_], writes=[pv_])
            nv_ = NPRE.v((b_ * 512, (b_ + 1) * 512))
            P.op("dve", lambda e, nv_=nv_, pv_=pv_: e.tensor_copy(out=nv_.ap, in_=pv_.ap), reads=[pv_], writes=[nv_])
        dma("pool", wga.v(), wga_d, "c_wga")
        dma("pool", wgx.v(), wgx_d, "c_wgx")
        P.op("pool", lambda e: e.iota(iota_i.v().ap, pattern=[[1, 128]], base=0, channel_multiplier=-1), writes=[iota_i.v()])
        P.op("dve", lambda e: e.tensor_copy(out=iota_f.v().ap, in_=iota_i.v().ap), reads=[iota_i.v()], writes=[iota_f.v()])
        P.op("dve", lambda e: e.tensor_single_scalar(out=ident.v().ap, in_=iota_f.v().ap, scalar=0.0, op=ALU.is_equal),
             reads=[iota_f.v()], writes=[ident.v()])
        P.op("dve", lambda e: e.memset(ones.v().ap, 1.0), writes=[ones.v()])
        P.op("dve", lambda e: e.memset(vcol("eps").ap, EPS), writes=[vcol("eps")])
        P.op("dve", lambda e: e.memset(vcol("one").ap, 1.0), writes=[vcol("one")])
        P.op("dve", lambda e: e.memset(XLP.v(None, (0, 4)).ap, 0.0), writes=[XLP.v(None, (0, 4))])
        act(vcol("tmp", 0, 8), vcol("lam", 0, 8), AF.Exp, scale=-1.0)
        act(vcol("tmp", 0, 8), vcol("tmp", 0, 8), AF.Ln, bias=vcol("one"))
        ts("dve", vcol("cexp", 0, 8), vcol("tmp", 0, 8), -8.0, ALU.mult)
        ts("dve", vcol("c2", 0, 8), vcol("tmp", 0, 8), -16.0, ALU.mult)
        ts("dve", vcol("ch", 0, 8), vcol("tmp", 0, 8), -4.0, ALU.mult)
        ts("dve", vcol("hga", 0, 8), vcol("bga", 0, 8), 0.5, ALU.mult)
        ts("dve", vcol("hgx", 0, 8), vcol("bgx", 0, 8), 0.5, ALU.mult)
        ts("dve", vcol("hbu", 0, 8), vcol("b_in", 24, 8), 0.5, ALU.mult)
        P.op("dve", lambda e: e.memset(vcol("qtr").ap, 0.25), writes=[vcol("qtr")])
        cdw_f = vecs.v((VC["cdw"], VC["cdw"] + 248))
        P.op("dve", lambda e: e.tensor_copy(out=cdwb.v((0, 248)).ap, in_=cdw_f.ap), reads=[cdw_f], writes=[cdwb.v((0, 248))])
        win_seq = []
        state = {"next_load": 0}

        def win_issue_loads(upto):
            while state["next_load"] < min(upto, len(win_seq)):
                q = state["next_load"]
                sl = q % NSLOT
                dma("pool", WIN[sl].v(), win_d[win_seq[q]], "win%d" % sl)
                state["next_load"] += 1

        pro_cnt = [0]

        def prologue_tasks(src_ap, dst, dst_col, ncols):
            i = pro_cnt[0]
            pro_cnt[0] += 1
            xs = XS[i % 4]
            hb = HNB[i % 3]
            sq = vcol("ssq", (i % 8) * 4, 1)
            sd = vcol("ssq", (i % 8) * 4 + 1, 1)
            rs = vcol("ssq", (i % 8) * 4 + 2, 1)

            def s0():
                if i > 0:
                    dma("sp", xs.v(), src_ap, "xs%d" % (i % 4))

            def s1():
                act(JK.v(), xs.v(), AF.Square, accum=sq)

            def s2():
                ts("dve", sd, sq, 1.0 / D, ALU.mult, EPS, ALU.add)
                act(sd, sd, AF.Sqrt)

            def s2b():
                P.op("dve", lambda e: e.reciprocal(out=rs.ap, in_=sd.ap), reads=[sd], writes=[rs])
                stt("dve", hb.v(), xs.v(), rs, NPRE.v(), ALU.mult, ALU.mult)

            def s3():
                def fn(e):
                    ins = None
                    for k in range(16):
                        ins = e.transpose(out=PST.v(k).ap, in_=hb.v((k * 128, (k + 1) * 128)).ap, identity=ident.v().ap)
                    return ins
                P.op("pe", fn, reads=[hb.v(), ident.v()], writes=[PST.v()])
                for (k0, k1, eng) in ((0, 8, "act"), (8, 16, "dve")):
                    src = PST.v((k0, k1), (0, ncols))
                    dv = dst.v((k0, k1), (dst_col, dst_col + ncols))
                    if eng == "act":
                        P.op("act", lambda e, src=src, dv=dv: e.activation(out=dv.ap, in_=src.ap, func=AF.Copy), reads=[src], writes=[dv])
                    else:
                        P.op("dve", lambda e, src=src, dv=dv: e.tensor_copy(out=dv.ap, in_=src.ap), reads=[src], writes=[dv])
            return [s0, s1, s2, s2b, s3]

        def pipeline_prologue(tiles):
            out, pos = [], []
            n = len(tiles)
            for step in range(-1, n + 3):
                if 0 <= step + 1 < n:
                    out.append(tiles[step + 1][0])
                if 0 <= step - 1 < n:
                    out.append(tiles[step - 1][2])
                if 0 <= step - 2 < n:
                    out.append(tiles[step - 2][3])
                if 0 <= step - 3 < n:
                    out.append(tiles[step - 3][4])
                    pos.append(len(out))
                if 0 <= step < n:
                    out.append(tiles[step][1])
            return out, pos

        bank_rr = [0]
        aux_rr = [0]
        NSPLIT = 4
        aux_list = [4, 5]

        def aux_bank():
            b = aux_list[aux_rr[0] % len(aux_list)]
            aux_rr[0] += 1
            return PSB[b]

        def inproj_tasks(col, src, tiles, evac):
            q = len(win_seq)
            win_seq.append(col)
            wslot = WIN[q % NSLOT]
            tasks = []
            for ti, (c0, n) in enumerate(tiles):
                def task(ti=ti, c0=c0, n=n):
                    if q == 0 and ti == 0:
                        win_issue_loads(NSLOT)
                    if n >= 128:
                        pb = PSB[bank_rr[0] % 4]
                        bank_rr[0] += 1
                    else:
                        pb = aux_bank()
                    pv = pb.v((0, n))
                    pairs = [(wslot.v(k).ap, src.v(k, (c0, c0 + n)).ap) for k in range(16)]
                    mm_group(pv, pairs, reads=[wslot.v(), src.v(None, (c0, c0 + n))])
                    evac(ti, pv, c0, n)
                    if ti == len(tiles) - 1:
                        win_issue_loads(q + NSLOT + 1)
                tasks.append(task)
            return tasks

        def inproj_split_tasks(col, src, evac):
            q = len(win_seq)
            win_seq.append(col)
            wslot = WIN[q % NSLOT]
            banks = {}

            def sub(tile, c0, s_):
                def task():
                    if q == 0 and tile == 0 and s_ == 0:
                        win_issue_loads(NSLOT)
                    if s_ == 0:
                        banks[tile] = PSB[bank_rr[0] % 4]
                        bank_rr[0] += 1
                    pb = banks[tile]
                    lo_, hi_ = c0 + 128 * s_, c0 + 128 * s_ + 128
                    pv = pb.v((128 * s_, 128 * s_ + 128))
                    pairs = [(wslot.v(k).ap, src.v(k, (lo_, hi_)).ap) for k in range(16)]
                    mm_group(pv, pairs, reads=[wslot.v(), src.v(None, (lo_, hi_))])
                    if s_ == 3:
                        evac(tile, pb.v((0, 512)), c0, 512)
                return task

            def task_c():
                pv = aux_bank().v((0, 16))
                pairs = [(wslot.v(k).ap, src.v(k, (1024, 1040)).ap) for k in range(16)]
                mm_group(pv, pairs, reads=[wslot.v(), src.v(None, (1024, 1040))])
                evac(2, pv, 1024, 16)
                win_issue_loads(q + NSLOT + 1)
            return [sub(0, 0, s_) for s_ in range(4)], [sub(1, 512, s_) for s_ in range(4)], [task_c]

        def xproj_tasks(j, prefix):
            bcol = vcol("b_in", j)
            if prefix:
                def evac_x(ti, pv, c0, n):
                    act(XLP.v(j, (3 + c0, 3 + c0 + n)), pv, AF.Identity, bias=bcol)
                    if ti == 1:
                        fx = XLP.v(j, (3 + 1021, 3 + 1024))
                        ts("dve", fx, fx, vcol("flag"), ALU.mult)
                if j < NSPLIT:
                    return inproj_split_tasks(j, HNTP, evac_x)
                return inproj_tasks(j, HNTP, [(0, 512), (512, 512), (1024, 16)], evac_x)

            def evac_x(ti, pv, c0, n):
                if ti == 0:
                    srcv = XLP.v(j, (3 + TP - 3, 3 + TP))
                    P.op("act", lambda e: e.activation(out=XLO.v(j, (0, 3)).ap, in_=srcv.ap, func=AF.Copy), reads=[srcv], writes=[XLO.v(j, (0, 3))])
                ts("dve", XLO.v(j, (3 + c0 - 32, 3 + c0 - 32 + n)), pv, bcol, ALU.add)
            return inproj_tasks(j, HNT, [(32, 512), (544, 512)], evac_x)

        def lru_chain(j, prefix):
            T = TP if prefix else TO
            p = j % 2
            XLb = XLP if prefix else XLO
            Rj, Ij, Aj, Hj, XCj, XCBj = Rb[p], Ib[p], Ab[p], Hb[p], XC[p], XCB[p]
            ttiles = [(0, 512), (512, 512), (1024, 16)] if prefix else [(0, 512), (512, 512)]
            CV, GT = [], []
            for (t0, n) in ttiles:
                def btask(t0=t0, n=n):
                    pv = aux_bank().v((0, n))
                    pairs = [(dg4.v(j * 4 + k).ap, XLb.v(j, (t0 + k, t0 + k + n)).ap) for k in range(4)]
                    mm_group(pv, pairs, reads=[dg4.v((j * 4, j * 4 + 4)), XLb.v(j, (t0, t0 + n + 3))])
                    ts("dve", XCj.v((t0, t0 + n)), pv, vcol("lcb", j), ALU.add)
                    act(XCBj.v((t0, t0 + n)), pv, AF.Identity, bias=vcol("lcb", j))
                CV.append(btask)
            for (t0, n) in ttiles:
                def ctask_r(t0=t0, n=n):
                    pv = aux_bank().v((0, n))
                    mm_group(pv, [(wga.v(j).ap, XCBj.v((t0, t0 + n)).ap)], reads=[wga.v(j), XCBj.v((t0, t0 + n))])
                    act(Rj.v((t0, t0 + n)), pv, AF.Tanh, bias=vcol("hga", j), scale=0.5)

                def ctask_i(t0=t0, n=n):
                    pv = aux_bank().v((0, n))
                    mm_group(pv, [(wgx.v(j).ap, XCBj.v((t0, t0 + n)).ap)], reads=[wgx.v(j), XCBj.v((t0, t0 + n))])
                    act(Ij.v((t0, t0 + n)), pv, AF.Tanh, bias=vcol("hgx", j), scale=0.5)
                    stt("dve", Ij.v((t0, t0 + n)), Ij.v((t0, t0 + n)), 1.0, XCj.v((t0, t0 + n)), ALU.add, ALU.mult)
                GT.append(ctask_r)
                GT.append(ctask_i)

            r = Rj.v((0, T)); i_ = Ij.v((0, T)); a = Aj.v((0, T)); h = Hj.v((0, T))

            def ch1():
                act(a, r, AF.Exp, scale=vcol("ch", j), bias=vcol("ch", j))
                act(r, r, AF.Exp, scale=vcol("cexp", j), bias=vcol("cexp", j))

            def ch2():
                act(r, r, AF.Sqrt, bias=vcol("qtr"), scale=-0.25)
                tt("dve", i_, i_, r, ALU.mult)

            def ch3():
                if prefix:
                    P.op("dve", lambda e: e.tensor_tensor_scan(out=Hj.v((0, 1024)).ap, data0=Aj.v((0, 1024)).ap, data1=Ij.v((0, 1024)).ap,
                                                             initial=0.0, op0=ALU.mult, op1=ALU.add),
                         reads=[Aj.v((0, 1024)), Ij.v((0, 1024))], writes=[Hj.v((0, 1024))])
                    ts("dve", vcol("hin", j), Hj.v((1023, 1024)), vcol("flag"), ALU.mult)
                    P.op("dve", lambda e: e.tensor_tensor_scan(out=Hj.v((1024, TP)).ap, data0=Aj.v((1024, TP)).ap, data1=Ij.v((1024, TP)).ap,
                                                             initial=vcol("hin", j).ap, op0=ALU.mult, op1=ALU.add),
                         reads=[Aj.v((1024, TP)), Ij.v((1024, TP)), vcol("hin", j)], writes=[Hj.v((1024, TP))])
                    P.op("dve", lambda e: e.tensor_copy(out=vcol("hst", j).ap, in_=Hj.v((TP - 1, TP)).ap),
                         reads=[Hj.v((TP - 1, TP))], writes=[vcol("hst", j)])
                else:
                    P.op("dve", lambda e: e.tensor_tensor_scan(out=h.ap, data0=a.ap, data1=i_.ap,
                                                             initial=vcol("hst", j).ap, op0=ALU.mult, op1=ALU.add),
                         reads=[a, i_, vcol("hst", j)], writes=[h])
                    tt("dve", YL.v(j), h, YL.v(j), ALU.mult)
            return CV, GT, [ch1, ch2, ch3]

        def chain_list(prefix):
            parts = [lru_chain(j, prefix) for j in range(8)]
            out = list(parts[0][0])
            for s_ in range(9):
                gt = parts[s_][1] if s_ < 8 else []
                cv = parts[s_ + 1][0] if s_ + 1 < 8 else []
                ch = parts[s_ - 1][2] if s_ >= 1 else []
                out += interleave(gt + cv, ch)
            return out

        def run_tasks(ts_):
            for t in ts_:
                t()

        def chk(name):
            if stop == name:
                raise _Stop()

        XCOL = list(range(0, 8))
        GLCOL = list(range(8, 16))
        UACOL = list(range(16, 24))
        UBCOL = list(range(24, 32))
        GCCOL = list(range(32, 40))

        try:
            pre_tiles = [prologue_tasks(xp_d[i * 128:(i + 1) * 128, :], HNTP, i * 128, 128 if i < 8 else 16) for i in range(9)]
            own_tiles = [prologue_tasks(xo_d[i * 128:(i + 1) * 128, :], HNT, HALO + i * 128, 128) for i in range(8)]
            pro_list, pro_pos = pipeline_prologue(pre_tiles + own_tiles)
            XP = [xproj_tasks(j, True) for j in range(8)]

            def g_tasks(j):
                def evac(ti, pv, c0, n):
                    act(YL.v(j, (c0 - HALO, c0 - HALO + n)), pv, AF.Silu, bias=vcol("b_in", 8 + j))
                return inproj_tasks(GLCOL[j], HNT, [(HALO, 512), (HALO + 512, 512)], evac)
            IP1 = []
            for j in range(8):
                IP1 += xproj_tasks(j, False)
                IP1 += g_tasks(j)
            UT = [(0, 352), (352, 352), (704, 352)]
            IP2 = []
            for j in range(8):
                def evac_b(ti, pv, c0, n, j=j):
                    act(SG1.v((c0, c0 + n)), pv, AF.Tanh, bias=vcol("hbu", j), scale=0.5)
                    if ti == 2:
                        ts("dve", SG1.v(), SG1.v(), 0.5, ALU.mult, 0.5, ALU.add)
                IP2 += inproj_tasks(UBCOL[j], HNT, UT, evac_b)

                def evac_a(ti, pv, c0, n, j=j):
                    stt("dve", V.v(j, (c0, c0 + n)), pv, vcol("b_in", 16 + j), SG1.v((c0, c0 + n)), ALU.add, ALU.mult)
                    if ti == 0:
                        fx = V.v(j, (0, 16))
                        ts("dve", fx, fx, vcol("flag"), ALU.mult)
                IP2 += inproj_tasks(UACOL[j], HNT, UT, evac_a)
            IP3 = []
            for j in range(8):
                def evac_g(ti, pv, c0, n, j=j):
                    act(YC.v(j, (c0 - HALO, c0 - HALO + n)), pv, AF.Silu, bias=vcol("b_in", 32 + j))
                IP3 += inproj_tasks(GCCOL[j], HNT, [(HALO, 512), (HALO + 512, 512)], evac_g)

            def sub_groups(t_):
                for j in range(NSPLIT):
                    if t_ < 4:
                        XP[j][0][t_]()
                    elif t_ < 8:
                        XP[j][1][t_ - 4]()
                    else:
                        XP[j][2][0]()

            prev_pos = 0
            for t_ in range(9):
                run_tasks(pro_list[prev_pos:pro_pos[t_]])
                prev_pos = pro_pos[t_]
                if t_ >= 1:
                    sub_groups(t_ - 1)
            run_tasks(pro_list[prev_pos:pro_pos[9]])
            prev_pos = pro_pos[9]
            sub_groups(8)
            lcw_v = vecs.v((VC["lcw"], VC["lcw"] + 32))
            P.op("dve", lambda e: e.tensor_tensor(
                out=dg4.v().ap, in0=ident.v().ap.unsqueeze(1).broadcast_to([128, 32, 128]),
                in1=lcw_v.ap.unsqueeze(2).broadcast_to([128, 32, 128]), op=ALU.mult),
                reads=[ident.v(), lcw_v], writes=[dg4.v()])
            rest_ip = [t for j in range(NSPLIT, 8) for t in XP[j]]
            run_tasks(interleave(pro_list[prev_pos:], rest_ip))
            dump("hnt_p", HNTP.v(), (128, 16, TP))
            P.op("act", lambda e: e.activation(out=HNT.v(None, (0, HALO)).ap, in_=HNTP.v(None, (TP - HALO, TP)).ap, func=AF.Copy),
                 reads=[HNTP.v(None, (TP - HALO, TP))], writes=[HNT.v(None, (0, HALO))])
            dump("hnt_o", HNT.v(), (128, 16, HALO + TO))
            aux_list.extend([6, 7])
            run_tasks(interleave(IP1, chain_list(True)))
            dump("hst", vecs.v((VC["hst"], VC["hst"] + 8)), (128, 8))
            chk("p0")
            run_tasks(IP2[:4])
            run_tasks(interleave(IP2[4:], chain_list(False)))
            dump("yl", YL.v(), (128, NCH, TO))
            dump("v", V.v(), (128, NCH, HALO + TO))
            chk("p1")
            built = {}

            def build_into(slot, c):
                wv = cdwb.v((c * 31, c * 31 + 31))
                P.op("dve", lambda e: e.tensor_tensor(
                    out=slot.v().ap, in0=ident.v().ap.unsqueeze(1).broadcast_to([128, 31, 128]),
                    in1=wv.ap.unsqueeze(2).broadcast_to([128, 31, 128]), op=ALU.mult),
                    reads=[ident.v(), wv], writes=[slot.v()])

            build_into(DG31X, 0)
            built[(0, 0)] = DG31X
            for j in range(8):
                run_tasks(IP3[2 * j:2 * j + 2])
                if j == 0:
                    for cc in range(8):
                        dma("pool", WOUT[cc].v(), wout_d[cc * 128:(cc + 1) * 128, :], "wout%d" % cc)
            for cc in range(8, 16):
                dma("pool", WOUT[cc].v(), wout_d[cc * 128:(cc + 1) * 128, :], "wout%d" % cc)
            for cc in range(8):
                dma("pool", PWc[cc].v(), pw_d[cc * 128:(cc + 1) * 128, :], "pw%d" % cc)

            dg_cnt = [0]
            rr = [0]
            pend_stats = [None]
            pmean, psq = PSB[6].v(), PSB[7].v()

            def dg_build(tt_, c):
                if (tt_, c) in built or c > 7:
                    return
                slot = DG31[dg_cnt[0] % 2]
                dg_cnt[0] += 1
                built[(tt_, c)] = slot
                build_into(slot, c)

            def conv_task(tt_, c):
                t0 = tt_ * 512
                dg_build(tt_, c)
                slot = built[(tt_, c)]
                pv = PSB[rr[0] % 6].v()
                rr[0] += 1
                pairs = [(slot.v(k).ap, V.v(c, (HALO + t0 + k - 30, HALO + t0 + k - 30 + 512)).ap) for k in range(31)]
                mm_group(pv, pairs, reads=[slot.v(), V.v(c, (HALO + t0 - 30, HALO + t0 + 512))])
                cb, c2b = CB[c % 2], C2B[c % 2]
                act(C.v(c), pv, AF.Identity, bias=vcol("cdb", c))
                act(cb.v(), pv, AF.Identity, bias=vcol("cdb", c))
                act(c2b.v(), pv, AF.Square, bias=vcol("cdb", c))

                def fn_m(e):
                    return e.matmul(pmean.ap, lhsT=ones.v().ap, rhs=cb.v().ap, start=(c == 0), stop=(c == 7))

                def fn_s(e):
                    return e.matmul(psq.ap, lhsT=ones.v().ap, rhs=c2b.v().ap, start=(c == 0), stop=(c == 7))
                def stats():
                    P.op("pe", fn_m, reads=[ones.v(), cb.v()], writes=[pmean])
                    P.op("pe", fn_s, reads=[ones.v(), c2b.v()], writes=[psq])
                prev = pend_stats[0]
                pend_stats[0] = stats
                if prev is not None:
                    prev()

            def flush_stats():
                if pend_stats[0] is not None:
                    pend_stats[0]()
                    pend_stats[0] = None

            def ln_stats():
                ts("dve", MU.v(), pmean, 1.0 / DC, ALU.mult)
                tt("dve", MSQ.v(), MU.v(), MU.v(), ALU.mult)
                stt("dve", MSQ.v(), psq, 1.0 / DC, MSQ.v(), ALU.mult, ALU.subtract)
                act(MSQ.v(), MSQ.v(), AF.Ln, bias=vcol("eps"))
                act(RSTD.v(), MSQ.v(), AF.Exp, scale=-0.5)

            def norm_task(c):
                eng = "dve"
                tt(eng, C.v(c), C.v(c), MU.v(), ALU.subtract)
                tt(eng, C.v(c), C.v(c), RSTD.v(), ALU.mult)
                act(S.v(c), C.v(c), AF.Silu, bias=vcol("lnb", c), scale=vcol("lnw", c))

            def pw_task(tt_, co):
                t0 = tt_ * 512
                pv = PSB[rr[0] % 6].v()
                rr[0] += 1
                pairs = [(PWc[ci].v((co * 128, (co + 1) * 128)).ap, S.v(ci).ap) for ci in range(8)]
                mm_group(pv, pairs, reads=[PWc[ci].v((co * 128, (co + 1) * 128)) for ci in range(8)] + [S.v()])
                ycv = YC.v(co, (t0, t0 + 512))
                stt("dve", ycv, pv, vcol("pwb", co), ycv, ALU.add, ALU.mult)

            for c in range(8):
                conv_task(0, c)
            dg_build(1, 0)
            dg_build(1, 1)
            flush_stats()
            dump("c", C.v(), (128, NCH, 512))
            ln_stats()
            for c in range(8):
                norm_task(c)
                if c == 7:
                    dump("s", S.v(), (128, NCH, 512))
                conv_task(1, c)
                dg_build(1, c + 2)
            flush_stats()
            for co in range(8):
                pw_task(0, co)
            ln_stats()

            dma("sp", NPOST.v(), npost_d, "c_npost")
            op_cnt = [0]

            def outproj_task(i):
                tok = i * 128
                xe, ot = XE[0], OUT[0]
                dma("sp", xe.v(), xo_d[tok:tok + 128, :], "xe0")
                base = (op_cnt[0] % 2) * 4
                op_cnt[0] += 1
                for dsl in range(4):
                    pv = PSB[base + dsl].v()
                    pairs = []
                    rd = []
                    for cc in range(16):
                        ysrc = YL if cc < 8 else YC
                        yv = ysrc.v(cc % 8, (tok, tok + 128))
                        wv = WOUT[cc].v((dsl * 512, (dsl + 1) * 512))
                        pairs.append((yv.ap, wv.ap))
                        rd += [yv, wv]
                    mm_group(pv, pairs, reads=rd)
                sb_ = VC["ssq"] + 32 + (i % 2) * 8
                sqc = vecs.v((sb_, sb_ + 4))
                for dsl in range(4):
                    act(junk.v((0, 512)), PSB[base + dsl].v(), AF.Square, accum=vecs.v((sb_ + dsl, sb_ + dsl + 1)))
                tot = vecs.v((sb_ + 4, sb_ + 5))
                sd = vecs.v((sb_ + 5, sb_ + 6))
                rs = vecs.v((sb_ + 6, sb_ + 7))
                P.op("dve", lambda e: e.tensor_reduce(out=tot.ap, in_=sqc.ap, axis=mybir.AxisListType.X, op=ALU.add),
                     reads=[sqc], writes=[tot])
                ts("dve", sd, tot, 1.0 / D, ALU.mult, EPS, ALU.add)
                act(sd, sd, AF.Sqrt)
                P.op("dve", lambda e: e.reciprocal(out=rs.ap, in_=sd.ap), reads=[sd], writes=[rs])
                for dsl in range(4):
                    sl = (dsl * 512, (dsl + 1) * 512)
                    stt("dve", ot.v(sl), PSB[base + dsl].v(), rs, NPOST.v(sl), ALU.mult, ALU.mult)
                    if dsl == 1:
                        h0 = (0, 1024)
                        tt("pool", ot.v(h0), ot.v(h0), xe.v(h0), ALU.add)
                        P.op("sp", lambda e: e.dma_start(out=out_d[tok:tok + 128, 0:1024], in_=ot.v(h0).ap),
                             reads=[ot.v(h0)], dma_key="out0a")
                h1 = (1024, 2048)
                tt("dve", ot.v(h1), ot.v(h1), xe.v(h1), ALU.add)
                P.op("sp", lambda e: e.dma_start(out=out_d[tok:tok + 128, 1024:2048], in_=ot.v(h1).ap),
                     reads=[ot.v(h1)], dma_key="out0b")

            for c in range(8):
                norm_task(c)
            for i in range(4):
                outproj_task(i)
            for co in range(8):
                pw_task(1, co)
            dump("yc", YC.v(), (128, NCH, TO))
            chk("p2")
            for i in range(4, 8):
                outproj_task(i)
        except _Stop:
            pass

        P.emit(stack)
    return nc, dbg_d


_CACHE = {}


def _layout(inp):
    f = np.float32
    x = np.asarray(inp["x"], f)
    meta = np.asarray(inp["meta_tokens"], f)
    w_in = np.asarray(inp["w_in"], f)[0]
    w_in_r = np.ascontiguousarray(w_in.reshape(16, 128, 40, 128).transpose(2, 1, 0, 3))

    def cols(v, n):
        return np.asarray(v, f).reshape(n, 128).T

    def bd(w):
        w = np.asarray(w, f)[0]
        o = np.zeros((128, 8, 128), f)
        for j in range(8):
            for a in range(2):
                o[64 * a:64 * a + 64, j, 64 * a:64 * a + 64] = w[2 * j + a]
        return o
    vec = np.zeros((128, 512), f)
    c = 0
    parts = [cols(inp["b_in"][0], 40), cols(inp["lru_conv_b"][0], 8), cols(inp["b_gate_a"][0], 8),
             cols(inp["b_gate_x"][0], 8), cols(inp["lru_lambda"][0], 8), cols(inp["conf_dw_b"][0], 8),
             cols(inp["conf_ln_w"][0], 8), cols(inp["conf_ln_b"][0], 8), cols(inp["conf_pw_b"][0], 8),
             np.asarray(inp["lru_conv_w"], f)[0].reshape(4, 8, 128).transpose(2, 1, 0).reshape(128, 32),
             np.asarray(inp["conf_dw_w"], f)[0].reshape(31, 8, 128).transpose(2, 1, 0).reshape(128, 248)]
    for p_ in parts:
        vec[:, c:c + p_.shape[1]] = p_
        c += p_.shape[1]
    shared = {
        "w_in_r": w_in_r,
        "w_out": np.ascontiguousarray(np.asarray(inp["w_out"], f)[0]),
        "pw_w": np.ascontiguousarray(np.asarray(inp["conf_pw_w"], f)[0]),
        "wga_bd": bd(inp["w_gate_a"]), "wgx_bd": bd(inp["w_gate_x"]),
        "pre_bc": np.ascontiguousarray(np.broadcast_to(np.asarray(inp["pre_norm_w"], f)[0][None, :], (128, D))),
        "post_bc": np.ascontiguousarray(np.broadcast_to(np.asarray(inp["post_norm_w"], f)[0][None, :], (128, D))),
        "vecs": vec,
    }
    maps = []
    for core in range(8):
        b, h = core // 2, core % 2
        xp = np.zeros((9 * 128, D), f)
        if h == 0:
            xp[1024:1040] = meta
        else:
            xp[0:16] = meta
            xp[16:1040] = x[b, 0:1024]
        m = dict(shared)
        m["xp"] = xp
        m["xo"] = np.ascontiguousarray(x[b, h * 1024:(h + 1) * 1024])
        m["flag"] = np.full((128, 1), float(h), f)
        maps.append(m)
    return maps


def kernel(**inputs):
    if "nc" not in _CACHE:
        _CACHE["nc"] = build_program(False)[0]
    nc = _CACHE["nc"]
    maps = _layout(inputs)
    res = run_bass_kernel_spmd(nc, maps, core_ids=list(range(8)))
    out = np.zeros((4, 2048, D), np.float32)
    for core in range(8):
        b, h = core // 2, core % 2
        out[b, h * 1024:(h + 1) * 1024] = res.results[core]["out"]
    return out
```

```python
import numpy as np
import concourse.bass as bass
import concourse.mybir as mybir
from concourse.bass_utils import run_bass_kernel_spmd

F32 = mybir.dt.float32
BF16 = mybir.dt.bfloat16
I32 = mybir.dt.int32
ALU = mybir.AluOpType
AF = mybir.ActivationFunctionType

D = 2048
DL = 1024
DC = 1024
NCH = 8
TP = 1040
TO = 1024
HALO = 32
EPS = 1e-6
SB_BYTES = 206 * 1024
DT_SIZE = {F32: 4, BF16: 2, I32: 4}


class View:
    __slots__ = ("ap", "space", "ivals")

    def __init__(self, ap, space, ivals):
        self.ap = ap
        self.space = space
        self.ivals = ivals


class Buf:
    def __init__(self, prog, space, off, dims, dt):
        self.prog = prog
        self.space = space
        self.off = off
        self.dims = tuple(dims)
        self.dt = dt
        self.esz = DT_SIZE[dt]
        n = 1
        for d in dims:
            n *= d
        self.nbytes = n * self.esz
        assert off % 4 == 0 and self.nbytes % 4 == 0, (off, dims)
        base = prog.arena[space]
        ap = base[:, off // 4:(off + self.nbytes) // 4]
        if dt != F32:
            ap = ap.bitcast(dt)
        if len(dims) == 2:
            ap = ap.rearrange("p (a b) -> p a b", a=dims[0])
        self.full = ap
        if space == "sb":
            assert off + self.nbytes <= SB_BYTES, ("sbuf overflow", off, self.nbytes)
        else:
            assert off + self.nbytes <= 16384

    def _iv(self, s, e):
        s = self.off + s * self.esz
        e = self.off + e * self.esz
        if self.space == "ps":
            s = (s // 2048) * 2048
            e = ((e + 2047) // 2048) * 2048
        return (s, e)

    def v(self, *idx, parts=None):
        dims = self.dims
        rng = []
        for i, d in enumerate(dims):
            x = idx[i] if i < len(idx) else None
            if x is None:
                rng.append((0, d, False))
            elif isinstance(x, int):
                rng.append((x, x + 1, True))
            else:
                rng.append((x[0], x[1], False))
        for (a, b, _), d in zip(rng, dims):
            assert 0 <= a < b <= d, (idx, dims)
        p = slice(None) if parts is None else slice(parts[0], parts[1])
        if len(dims) == 1:
            a, b, _ = rng[0]
            ap = self.full[p, a:b]
            iv = [self._iv(a, b)]
        else:
            (a0, b0, s0), (a1, b1, _) = rng
            if s0:
                ap = self.full[p, a0, a1:b1]
            else:
                ap = self.full[p, a0:b0, a1:b1]
            if a1 == 0 and b1 == dims[1]:
                iv = [self._iv(a0 * dims[1], b0 * dims[1])]
            else:
                iv = [self._iv(k * dims[1] + a1, k * dims[1] + b1) for k in range(a0, b0)]
        return View(ap, self.space, iv)


class Op:
    __slots__ = ("eng", "fn", "waits", "done", "clock", "idx", "is_dma")


class Prog:
    ENGS = ("pe", "act", "dve", "pool", "sp")

    def __init__(self, nc, sb, ps):
        self.nc = nc
        self.arena = {"sb": sb, "ps": ps}
        self.ops = []
        self.streams = {e: [] for e in self.ENGS}
        self.eng_count = {e: 0 for e in self.ENGS}
        self.clock = {e: {} for e in self.ENGS}
        self.dma_count = {}
        self.recs = {"sb": [], "ps": [], "dram": []}
        self.sb_off = 0

    def alloc(self, dims, dt, off=None, space="sb"):
        if off is None:
            off = self.sb_off
            b = Buf(self, space, off, dims, dt)
            self.sb_off = off + ((b.nbytes + 31) // 32) * 32
            return b
        return Buf(self, space, off, dims, dt)

    def _deps(self, reads, writes):
        deps = set()
        for vw in reads:
            recs = self.recs[vw.space]
            for (s, e) in vw.ivals:
                for r in recs:
                    if r[3] and r[0] < e and s < r[1]:
                        deps.add(r[2])
        for vw in writes:
            recs = self.recs[vw.space]
            for (s, e) in vw.ivals:
                for r in recs:
                    if r[0] < e and s < r[1]:
                        deps.add(r[2])
        return deps

    def _record(self, op_id, eng, reads, writes):
        for vw in writes:
            for (s, e) in vw.ivals:
                recs = self.recs[vw.space]
                recs[:] = [r for r in recs if not (s <= r[0] and r[1] <= e)]
                recs.append((s, e, op_id, True, eng))
        for vw in reads:
            for (s, e) in vw.ivals:
                recs = self.recs[vw.space]
                recs[:] = [r for r in recs if not (r[0] == s and r[1] == e and (not r[3]) and r[4] == eng)]
                recs.append((s, e, op_id, False, eng))

    def op(self, eng, fn, reads=(), writes=(), dma_key=None, extra_deps=()):
        reads = [r for r in reads if r is not None]
        writes = [w for w in writes if w is not None]
        writes = writes + [r for r in reads if r.space == "ps"]
        reads = [r for r in reads if r.space != "ps"]
        deps = self._deps(reads, writes)
        deps.update(extra_deps)
        o = Op()
        o.eng = eng
        o.fn = fn
        o.is_dma = dma_key is not None
        clk = self.clock[eng]
        waits = []
        for d in sorted(deps):
            dop = self.ops[d]
            k, v = dop.done
            if clk.get(k, 0) >= v:
                continue
            waits.append((k, v))
            for kk, vv in dop.clock.items():
                if clk.get(kk, 0) < vv:
                    clk[kk] = vv
        best = {}
        for k, v in waits:
            best[k] = max(best.get(k, 0), v)
        o.waits = sorted(best.items())
        if dma_key is not None:
            n = self.dma_count.get(dma_key, 0) + 1
            self.dma_count[dma_key] = n
            o.done = ("dma:" + dma_key, 16 * n)
        else:
            self.eng_count[eng] += 1
            o.done = (eng, self.eng_count[eng])
        o.clock = dict(clk)
        o.clock[o.done[0]] = o.done[1]
        o.idx = len(self.ops)
        self.ops.append(o)
        self.streams[eng].append(o)
        self._record(o.idx, eng, reads, writes)
        return o.idx

    def emit(self, stack):
        nc = self.nc
        sems = {}
        keys = list(self.ENGS[:4]) + sorted("dma:" + k for k in self.dma_count)
        for k in keys:
            sems[k] = stack.enter_context(nc.semaphore("s_" + k.replace(":", "_")))
        block = stack.enter_context(nc.Block())

        def run(engine_obj, stream, final_waits=()):
            for o in stream:
                for (k, v) in o.waits:
                    engine_obj.wait_ge(sems[k], v)
                ins = o.fn(engine_obj)
                ins.then_inc(sems[o.done[0]], 16 if o.is_dma else 1)
            for (k, v) in final_waits:
                engine_obj.wait_ge(sems[k], v)

        final = [("dma:" + k, 16 * n) for k, n in self.dma_count.items()]

        @block.tensor
        def _(e):
            run(e, self.streams["pe"])

        @block.scalar
        def _(e):
            run(e, self.streams["act"])

        @block.vector
        def _(e):
            run(e, self.streams["dve"])

        @block.gpsimd
        def _(e):
            run(e, self.streams["pool"])

        @block.sync
        def _(e):
            run(e, self.streams["sp"], final)


def interleave(*lists):
    lists = [l for l in lists if l]
    out = []
    if not lists:
        return out
    n = max(len(l) for l in lists)
    pos = [0] * len(lists)
    for step in range(1, n + 1):
        for i, l in enumerate(lists):
            tgt = (step * len(l) + n - 1) // n
            while pos[i] < tgt:
                out.append(l[pos[i]])
                pos[i] += 1
    return out


class _Stop(Exception):
    pass


def build_program(dbg=False, stop=None):
    import contextlib
    nc = bass.Bass("TRN2", target_bir_lowering=False)
    dram = {}

    def din(name, shape):
        dram[name] = nc.dram_tensor(name, list(shape), F32, kind="ExternalInput").ap()
        return dram[name]

    xp_d = din("xp", (9 * 128, D))
    xo_d = din("xo", (TO, D))
    flag_d = din("flag", (128, 1))
    win_d = din("w_in_r", (40, 128, 16, 128))
    wout_d = din("w_out", (D, D))
    pw_d = din("pw_w", (DC, DC))
    wga_d = din("wga_bd", (128, NCH, 128))
    wgx_d = din("wgx_bd", (128, NCH, 128))
    npre_d = din("pre_bc", (128, D))
    npost_d = din("post_bc", (128, D))
    vec_d = din("vecs", (128, 512))
    out_d = nc.dram_tensor("out", [TO, D], F32, kind="ExternalOutput").ap()
    dbg_d = {}

    VC = {}
    c = 0
    for nm, w in (("b_in", 40), ("lcb", 8), ("bga", 8), ("bgx", 8), ("lam", 8), ("cdb", 8),
                  ("lnw", 8), ("lnb", 8), ("pwb", 8), ("lcw", 32), ("cdw", 248)):
        VC[nm] = c
        c += w
    VC["cexp"] = c; c += 8
    VC["c2"] = c; c += 8
    VC["ch"] = c; c += 8
    VC["hga"] = c; c += 8
    VC["hgx"] = c; c += 8
    VC["hbu"] = c; c += 8
    VC["qtr"] = c; c += 1
    VC["tmp"] = c; c += 8
    VC["eps"] = c; c += 1
    VC["one"] = c; c += 1
    VC["flag"] = c; c += 1
    VC["hst"] = c; c += 8
    VC["hin"] = c; c += 8
    VC["ssq"] = c; c += 48
    assert c <= 512

    stack = contextlib.ExitStack()
    with stack:
        sb = stack.enter_context(nc.sbuf_tensor("arena", [128, SB_BYTES // 4], F32))
        ps = stack.enter_context(nc.psum_tensor("psum", [128, 4096], F32))
        P = Prog(nc, sb, ps)

        vecs = P.alloc((512,), F32)
        ident = P.alloc((128,), BF16)
        ones = P.alloc((128,), BF16)
        wga = P.alloc((NCH, 128), BF16)
        wgx = P.alloc((NCH, 128), BF16)
        cdwb = P.alloc((256,), BF16)
        iota_i = P.alloc((128,), I32)
        iota_f = P.alloc((128,), F32)
        CB = [P.alloc((512,), BF16) for _ in range(2)]
        C2B = [P.alloc((512,), BF16) for _ in range(2)]
        yl0 = P.sb_off
        YL = P.alloc((NCH, TO), BF16)
        YC = P.alloc((NCH, TO), BF16)
        P.alloc((256,), BF16)
        HNTP = P.alloc((16, TP), BF16, off=yl0)
        assert HNTP.nbytes <= P.sb_off - yl0
        v0 = P.sb_off
        V = P.alloc((NCH, HALO + TO), BF16)
        XLP = P.alloc((NCH, TP + 4), BF16, off=v0)
        assert XLP.nbytes <= V.nbytes
        XLO = P.alloc((NCH, TO + 4), BF16, off=yl0 + 16384)
        assert yl0 + 16384 + XLO.nbytes <= v0
        dg4 = P.alloc((NCH * 4, 128), BF16)
        g0 = P.sb_off
        SG1 = P.alloc((HALO + TO,), F32)
        MU = P.alloc((512,), F32)
        RSTD = P.alloc((512,), F32)
        MSQ = P.alloc((512,), F32)
        NPRE = P.alloc((D,), F32, off=g0)
        DG31X = P.alloc((31, 128), BF16, off=g0)
        assert DG31X.nbytes <= 4224 + 3 * 2048
        junk = CB[0]
        y0 = P.sb_off
        NSLOT = 5
        HNT = P.alloc((16, HALO + TO), BF16, off=y0)
        WIN = [P.alloc((16, 128), BF16, off=y0 + HNT.nbytes + i * 4096) for i in range(NSLOT)]
        ysz = HNT.nbytes + NSLOT * 4096
        C = P.alloc((NCH, 512), F32, off=y0)
        S = P.alloc((NCH, 512), BF16, off=y0 + C.nbytes)
        DG31 = [P.alloc((31, 128), BF16, off=y0 + C.nbytes + S.nbytes + i * 7936) for i in range(2)]
        pwy = y0 + C.nbytes + S.nbytes + 2 * 7936
        assert pwy + 4 * 2048 <= y0 + ysz
        XE = [P.alloc((D,), F32, off=v0)]
        OUT = [P.alloc((D,), F32, off=v0 + 8192)]
        NPOST = P.alloc((D,), F32, off=y0 + C.nbytes + S.nbytes)
        x0 = y0 + ysz
        XS = [P.alloc((D,), F32, off=x0 + i * 8192) for i in range(2)]
        HNB = [P.alloc((D,), BF16, off=x0 + 16384 + i * 4096) for i in range(2)]
        JK = P.alloc((D,), BF16, off=x0 + 24576)
        lo = x0 + 28672
        TL = TP
        Rb = [P.alloc((TL,), F32, off=lo + (0 + i) * 4160) for i in range(2)]
        Ib = [P.alloc((TL,), F32, off=lo + (2 + i) * 4160) for i in range(2)]
        Ab = [P.alloc((TL,), F32, off=lo + (4 + i) * 4160) for i in range(2)]
        Hb = [P.alloc((TL,), F32, off=lo + (6 + i) * 4160) for i in range(2)]
        XC = [P.alloc((TL,), F32, off=lo + (8 + i) * 4160) for i in range(2)]
        XCB = [P.alloc((TL,), BF16, off=lo + 10 * 4160 + i * 2112) for i in range(2)]
        XS += [P.alloc((D,), F32, off=lo + i * 8192) for i in range(2)]
        HNB += [P.alloc((D,), BF16, off=lo + 16384)]
        xsz = 28672 + 10 * 4160 + 2 * 2112
        WOUT = [P.alloc((D,), BF16, off=x0 + i * 4096) for i in range(16)]
        assert xsz >= 16 * 4096 + 4 * 2048
        PWc = [P.alloc((DC,), BF16, off=x0 + 65536 + i * 2048) for i in range(4)] + \
              [P.alloc((DC,), BF16, off=pwy + i * 2048) for i in range(4)]
        assert x0 + xsz <= SB_BYTES, (x0 + xsz, SB_BYTES)
        PSB = [P.alloc((512,), F32, off=b * 2048, space="ps") for b in range(8)]
        PST = P.alloc((16, 128), BF16, off=6 * 2048, space="ps")

        def vcol(nm, j=0, n=1):
            return vecs.v((VC[nm] + j, VC[nm] + j + n))

        def dma(eng, out_v, in_ap, key, reads=()):
            return P.op(eng, lambda e: e.dma_start(out=out_v.ap, in_=in_ap), reads=reads, writes=[out_v], dma_key=key)

        def dump(name, view, shape):
            if not dbg:
                return
            if name not in dbg_d:
                dbg_d[name] = nc.dram_tensor("dbg_" + name, list(shape), view.ap.dtype, kind="ExternalOutput").ap()
            P.op("sp", lambda e: e.dma_start(out=dbg_d[name], in_=view.ap), reads=[view], dma_key="out_dbg_" + name)

        def act(out_v, in_v, func, bias=None, scale=None, accum=None, extra_r=()):
            kw = {}
            rd = [in_v] + list(extra_r)
            if bias is not None:
                if isinstance(bias, View):
                    kw["bias"] = bias.ap
                    rd.append(bias)
                else:
                    kw["bias"] = bias
            if scale is not None:
                if isinstance(scale, View):
                    kw["scale"] = scale.ap
                    rd.append(scale)
                else:
                    kw["scale"] = scale
            wr = [out_v]
            if accum is not None:
                kw["accum_out"] = accum.ap
                wr.append(accum)
            return P.op("act", lambda e: e.activation(out=out_v.ap, in_=in_v.ap, func=func, **kw), reads=rd, writes=wr)

        def tt(eng, out_v, a, b, op):
            return P.op(eng, lambda e: e.tensor_tensor(out=out_v.ap, in0=a.ap, in1=b.ap, op=op), reads=[a, b], writes=[out_v])

        def ts(eng, out_v, a, s1, op0, s2=None, op1=None):
            rd = [a]
            s1a = s1.ap if isinstance(s1, View) else s1
            s2a = s2.ap if isinstance(s2, View) else s2
            if isinstance(s1, View):
                rd.append(s1)
            if isinstance(s2, View):
                rd.append(s2)
            if op1 is None:
                return P.op(eng, lambda e: e.tensor_scalar(out=out_v.ap, in0=a.ap, scalar1=s1a, scalar2=None, op0=op0), reads=rd, writes=[out_v])
            return P.op(eng, lambda e: e.tensor_scalar(out=out_v.ap, in0=a.ap, scalar1=s1a, scalar2=s2a, op0=op0, op1=op1), reads=rd, writes=[out_v])

        def stt(eng, out_v, a, s, b, op0, op1):
            rd = [a, b]
            sa = s.ap if isinstance(s, View) else s
            if isinstance(s, View):
                rd.append(s)
            return P.op(eng, lambda e: e.scalar_tensor_tensor(out=out_v.ap, in0=a.ap, scalar=sa, in1=b.ap, op0=op0, op1=op1), reads=rd, writes=[out_v])

        def mm_group(out_v, pairs, reads):
            def fn(e):
                ins = None
                n = len(pairs)
                for i, (l, r) in enumerate(pairs):
                    ins = e.matmul(out_v.ap, lhsT=l, rhs=r, start=(i == 0), stop=(i == n - 1))
                return ins
            return P.op("pe", fn, reads=reads, writes=[out_v])

        NROW = XS[3].v(parts=(0, 1))
        dma("sp", NROW, npre_d[0:1, :], "c_npre")
        dma("sp", XS[0].v(), xp_d[0:128, :], "xs0")
        dma("sp", vecs.v((0, VC["cexp"])), vec_d[:, 0:VC["cexp"]], "c_vecs")
        dma("sp", vecs.v((VC["flag"], VC["flag"] + 1)), flag_d, "c_flag")
        ONESF = P.alloc((128,), F32, off=JK.off)
        P.op("dve", lambda e: e.memset(ONESF.v().ap, 1.0), writes=[ONESF.v()])
        for b_ in range(4):
            pv_ = PSB[b_].v()
            rhs_ = View(NROW.ap[:, b_ * 512:(b_ + 1) * 512], NROW.space, NROW.ivals)
            P.op("pe", lambda e, pv_=pv_, rhs_=rhs_: e.matmul(pv_.ap, lhsT=ONESF.v(parts=(0, 1)).ap, rhs=rhs_.ap, start=True, stop=True),
                 reads=[ONESF.v(), rhs_], writes=[pv_])
            nv_ = NPRE.v((b_ * 512, (b_ + 1) * 512))
            P.op("dve", lambda e, nv_=nv_, pv_=pv_: e.tensor_copy(out=nv_.ap, in_=pv_.ap), reads=[pv_], writes=[nv_])
        dma("pool", wga.v(), wga_d, "c_wga")
        dma("pool", wgx.v(), wgx_d, "c_wgx")
        P.op("pool", lambda e: e.iota(iota_i.v().ap, pattern=[[1, 128]], base=0, channel_multiplier=-1), writes=[iota_i.v()])
        P.op("dve", lambda e: e.tensor_copy(out=iota_f.v().ap, in_=iota_i.v().ap), reads=[iota_i.v()], writes=[iota_f.v()])
        P.op("dve", lambda e: e.tensor_single_scalar(out=ident.v().ap, in_=iota_f.v().ap, scalar=0.0, op=ALU.is_equal),
             reads=[iota_f.v()], writes=[ident.v()])
        P.op("dve", lambda e: e.memset(ones.v().ap, 1.0), writes=[ones.v()])
        P.op("dve", lambda e: e.memset(vcol("eps").ap, EPS), writes=[vcol("eps")])
        P.op("dve", lambda e: e.memset(vcol("one").ap, 1.0), writes=[vcol("one")])
        P.op("dve", lambda e: e.memset(XLP.v(None, (0, 4)).ap, 0.0), writes=[XLP.v(None, (0, 4))])
        def setup_consts():
            act(vcol("tmp", 0, 8), vcol("lam", 0, 8), AF.Exp, scale=-1.0)
            act(vcol("tmp", 0, 8), vcol("tmp", 0, 8), AF.Ln, bias=vcol("one"))
            ts("dve", vcol("cexp", 0, 8), vcol("tmp", 0, 8), -8.0, ALU.mult)
            ts("dve", vcol("c2", 0, 8), vcol("tmp", 0, 8), -16.0, ALU.mult)
            ts("dve", vcol("ch", 0, 8), vcol("tmp", 0, 8), -4.0, ALU.mult)
            ts("dve", vcol("hga", 0, 8), vcol("bga", 0, 8), 0.5, ALU.mult)
            ts("dve", vcol("hgx", 0, 8), vcol("bgx", 0, 8), 0.5, ALU.mult)
            ts("dve", vcol("hbu", 0, 8), vcol("b_in", 24, 8), 0.5, ALU.mult)
            P.op("dve", lambda e: e.memset(vcol("qtr").ap, 0.25), writes=[vcol("qtr")])
            cdw_f = vecs.v((VC["cdw"], VC["cdw"] + 248))
            P.op("dve", lambda e: e.tensor_copy(out=cdwb.v((0, 248)).ap, in_=cdw_f.ap), reads=[cdw_f], writes=[cdwb.v((0, 248))])

        win_seq = []
        state = {"next_load": 0}

        def win_issue_loads(upto):
            while state["next_load"] < min(upto, len(win_seq)):
                q = state["next_load"]
                sl = q % NSLOT
                dma("pool", WIN[sl].v(), win_d[win_seq[q]], "win%d" % sl)
                state["next_load"] += 1

        pro_cnt = [0]

        def prologue_tasks(src_ap, dst, dst_col, ncols):
            i = pro_cnt[0]
            pro_cnt[0] += 1
            xs = XS[i % 4]
            hb = HNB[i % 3]
            sq = vcol("ssq", (i % 8) * 4, 1)
            sd = vcol("ssq", (i % 8) * 4 + 1, 1)
            rs = vcol("ssq", (i % 8) * 4 + 2, 1)

            def s0():
                if i > 0:
                    dma("sp", xs.v(), src_ap, "xs%d" % (i % 4))

            def s1():
                act(JK.v(), xs.v(), AF.Square, accum=sq)

            def s2():
                ts("dve", sd, sq, 1.0 / D, ALU.mult, EPS, ALU.add)
                act(sd, sd, AF.Sqrt)

            def s2b():
                P.op("dve", lambda e: e.reciprocal(out=rs.ap, in_=sd.ap), reads=[sd], writes=[rs])
                stt("dve", hb.v(), xs.v(), rs, NPRE.v(), ALU.mult, ALU.mult)

            def s3():
                def fn(e):
                    ins = None
                    for k in range(16):
                        ins = e.transpose(out=PST.v(k).ap, in_=hb.v((k * 128, (k + 1) * 128)).ap, identity=ident.v().ap)
                    return ins
                P.op("pe", fn, reads=[hb.v(), ident.v()], writes=[PST.v()])
                for (k0, k1, eng) in ((0, 8, "act"), (8, 16, "dve")):
                    src = PST.v((k0, k1), (0, ncols))
                    dv = dst.v((k0, k1), (dst_col, dst_col + ncols))
                    if eng == "act":
                        P.op("act", lambda e, src=src, dv=dv: e.activation(out=dv.ap, in_=src.ap, func=AF.Copy), reads=[src], writes=[dv])
                    else:
                        P.op("dve", lambda e, src=src, dv=dv: e.tensor_copy(out=dv.ap, in_=src.ap), reads=[src], writes=[dv])
            return [s0, s1, s2, s2b, s3]

        def pipeline_prologue(tiles):
            out, pos = [], []
            n = len(tiles)
            for step in range(-1, n + 3):
                if 0 <= step + 1 < n:
                    out.append(tiles[step + 1][0])
                if 0 <= step < n:
                    out.append(tiles[step][1])
                if 0 <= step - 1 < n:
                    out.append(tiles[step - 1][2])
                if 0 <= step - 2 < n:
                    out.append(tiles[step - 2][3])
                if 0 <= step - 3 < n:
                    out.append(tiles[step - 3][4])
                    pos.append(len(out))
            return out, pos

        bank_rr = [0]
        aux_rr = [0]
        NSPLIT = 4
        aux_list = [4, 5]

        def aux_bank():
            b = aux_list[aux_rr[0] % len(aux_list)]
            aux_rr[0] += 1
            return PSB[b]

        def inproj_tasks(col, src, tiles, evac):
            q = len(win_seq)
            win_seq.append(col)
            wslot = WIN[q % NSLOT]
            tasks = []
            for ti, (c0, n) in enumerate(tiles):
                def task(ti=ti, c0=c0, n=n):
                    if q == 0 and ti == 0:
                        win_issue_loads(NSLOT)
                    if n >= 128:
                        pb = PSB[bank_rr[0] % 4]
                        bank_rr[0] += 1
                    else:
                        pb = aux_bank()
                    pv = pb.v((0, n))
                    pairs = [(wslot.v(k).ap, src.v(k, (c0, c0 + n)).ap) for k in range(16)]
                    mm_group(pv, pairs, reads=[wslot.v(), src.v(None, (c0, c0 + n))])
                    evac(ti, pv, c0, n)
                    if ti == len(tiles) - 1:
                        win_issue_loads(q + NSLOT + 1)
                tasks.append(task)
            return tasks

        def inproj_split_tasks(col, src, evac):
            q = len(win_seq)
            win_seq.append(col)
            wslot = WIN[q % NSLOT]
            banks = {}

            def sub(tile, c0, s_):
                def task():
                    if q == 0 and tile == 0 and s_ == 0:
                        win_issue_loads(NSLOT)
                    if s_ == 0:
                        banks[tile] = PSB[bank_rr[0] % 4]
                        bank_rr[0] += 1
                    pb = banks[tile]
                    lo_, hi_ = c0 + 128 * s_, c0 + 128 * s_ + 128
                    pv = pb.v((128 * s_, 128 * s_ + 128))
                    pairs = [(wslot.v(k).ap, src.v(k, (lo_, hi_)).ap) for k in range(16)]
                    mm_group(pv, pairs, reads=[wslot.v(), src.v(None, (lo_, hi_))])
                    if s_ == 3:
                        evac(tile, pb.v((0, 512)), c0, 512)
                return task

            def task_c():
                pv = aux_bank().v((0, 16))
                pairs = [(wslot.v(k).ap, src.v(k, (1024, 1040)).ap) for k in range(16)]
                mm_group(pv, pairs, reads=[wslot.v(), src.v(None, (1024, 1040))])
                evac(2, pv, 1024, 16)
                win_issue_loads(q + NSLOT + 1)
            return [sub(0, 0, s_) for s_ in range(4)], [sub(1, 512, s_) for s_ in range(4)], [task_c]

        def xproj_tasks(j, prefix):
            bcol = vcol("b_in", j)
            if prefix:
                def evac_x(ti, pv, c0, n):
                    act(XLP.v(j, (3 + c0, 3 + c0 + n)), pv, AF.Identity, bias=bcol)
                    if ti == 1:
                        fx = XLP.v(j, (3 + 1021, 3 + 1024))
                        ts("dve", fx, fx, vcol("flag"), ALU.mult)
                if j < NSPLIT:
                    return inproj_split_tasks(j, HNTP, evac_x)
                return inproj_tasks(j, HNTP, [(0, 512), (512, 512), (1024, 16)], evac_x)

            def evac_x(ti, pv, c0, n):
                if ti == 0:
                    srcv = XLP.v(j, (3 + TP - 3, 3 + TP))
                    P.op("act", lambda e: e.activation(out=XLO.v(j, (0, 3)).ap, in_=srcv.ap, func=AF.Copy), reads=[srcv], writes=[XLO.v(j, (0, 3))])
                ts("dve", XLO.v(j, (3 + c0 - 32, 3 + c0 - 32 + n)), pv, bcol, ALU.add)
            return inproj_tasks(j, HNT, [(32, 512), (544, 512)], evac_x)

        def lru_chain(j, prefix):
            T = TP if prefix else TO
            p = j % 2
            XLb = XLP if prefix else XLO
            Rj, Ij, Aj, Hj, XCj, XCBj = Rb[p], Ib[p], Ab[p], Hb[p], XC[p], XCB[p]
            ttiles = [(0, 512), (512, 512), (1024, 16)] if prefix else [(0, 512), (512, 512)]
            CV, GT = [], []
            for (t0, n) in ttiles:
                def btask(t0=t0, n=n):
                    pv = aux_bank().v((0, n))
                    pairs = [(dg4.v(j * 4 + k).ap, XLb.v(j, (t0 + k, t0 + k + n)).ap) for k in range(4)]
                    mm_group(pv, pairs, reads=[dg4.v((j * 4, j * 4 + 4)), XLb.v(j, (t0, t0 + n + 3))])
                    ts("dve", XCj.v((t0, t0 + n)), pv, vcol("lcb", j), ALU.add)
                    act(XCBj.v((t0, t0 + n)), pv, AF.Identity, bias=vcol("lcb", j))
                CV.append(btask)
            for (t0, n) in ttiles:
                def ctask_r(t0=t0, n=n):
                    pv = aux_bank().v((0, n))
                    mm_group(pv, [(wga.v(j).ap, XCBj.v((t0, t0 + n)).ap)], reads=[wga.v(j), XCBj.v((t0, t0 + n))])
                    act(Rj.v((t0, t0 + n)), pv, AF.Tanh, bias=vcol("hga", j), scale=0.5)

                def ctask_i(t0=t0, n=n):
                    pv = aux_bank().v((0, n))
                    mm_group(pv, [(wgx.v(j).ap, XCBj.v((t0, t0 + n)).ap)], reads=[wgx.v(j), XCBj.v((t0, t0 + n))])
                    act(Ij.v((t0, t0 + n)), pv, AF.Tanh, bias=vcol("hgx", j), scale=0.5)
                    stt("dve", Ij.v((t0, t0 + n)), Ij.v((t0, t0 + n)), 1.0, XCj.v((t0, t0 + n)), ALU.add, ALU.mult)
                GT.append(ctask_r)
                GT.append(ctask_i)

            r = Rj.v((0, T)); i_ = Ij.v((0, T)); a = Aj.v((0, T)); h = Hj.v((0, T))

            def ch1():
                act(a, r, AF.Exp, scale=vcol("ch", j), bias=vcol("ch", j))
                act(r, r, AF.Exp, scale=vcol("cexp", j), bias=vcol("cexp", j))

            def ch2():
                act(r, r, AF.Sqrt, bias=vcol("qtr"), scale=-0.25)
                tt("dve", i_, i_, r, ALU.mult)

            def ch3():
                if prefix:
                    P.op("dve", lambda e: e.tensor_tensor_scan(out=Hj.v((0, 1024)).ap, data0=Aj.v((0, 1024)).ap, data1=Ij.v((0, 1024)).ap,
                                                             initial=0.0, op0=ALU.mult, op1=ALU.add),
                         reads=[Aj.v((0, 1024)), Ij.v((0, 1024))], writes=[Hj.v((0, 1024))])
                    ts("dve", vcol("hin", j), Hj.v((1023, 1024)), vcol("flag"), ALU.mult)
                    P.op("dve", lambda e: e.tensor_tensor_scan(out=Hj.v((1024, TP)).ap, data0=Aj.v((1024, TP)).ap, data1=Ij.v((1024, TP)).ap,
                                                             initial=vcol("hin", j).ap, op0=ALU.mult, op1=ALU.add),
                         reads=[Aj.v((1024, TP)), Ij.v((1024, TP)), vcol("hin", j)], writes=[Hj.v((1024, TP))])
                    P.op("dve", lambda e: e.tensor_copy(out=vcol("hst", j).ap, in_=Hj.v((TP - 1, TP)).ap),
                         reads=[Hj.v((TP - 1, TP))], writes=[vcol("hst", j)])
                else:
                    P.op("dve", lambda e: e.tensor_tensor_scan(out=h.ap, data0=a.ap, data1=i_.ap,
                                                             initial=vcol("hst", j).ap, op0=ALU.mult, op1=ALU.add),
                         reads=[a, i_, vcol("hst", j)], writes=[h])
                    tt("dve", YL.v(j), h, YL.v(j), ALU.mult)
            return CV, GT, [ch1, ch2, ch3]

        def chain_list(prefix):
            parts = [lru_chain(j, prefix) for j in range(8)]
            out = list(parts[0][0])
            for s_ in range(9):
                gt = parts[s_][1] if s_ < 8 else []
                cv = parts[s_ + 1][0] if s_ + 1 < 8 else []
                ch = parts[s_ - 1][2] if s_ >= 1 else []
                out += interleave(gt + cv, ch)
            return out

        def run_tasks(ts_):
            for t in ts_:
                t()

        def chk(name):
            if stop == name:
                raise _Stop()

        XCOL = list(range(0, 8))
        GLCOL = list(range(8, 16))
        UACOL = list(range(16, 24))
        UBCOL = list(range(24, 32))
        GCCOL = list(range(32, 40))

        try:
            pre_tiles = [prologue_tasks(xp_d[i * 128:(i + 1) * 128, :], HNTP, i * 128, 128 if i < 8 else 16) for i in range(9)]
            own_tiles = [prologue_tasks(xo_d[i * 128:(i + 1) * 128, :], HNT, HALO + i * 128, 128) for i in range(8)]
            pro_list, pro_pos = pipeline_prologue(pre_tiles + own_tiles)
            XP = [xproj_tasks(j, True) for j in range(8)]

            def g_tasks(j):
                def evac(ti, pv, c0, n):
                    act(YL.v(j, (c0 - HALO, c0 - HALO + n)), pv, AF.Silu, bias=vcol("b_in", 8 + j))
                return inproj_tasks(GLCOL[j], HNT, [(HALO, 512), (HALO + 512, 512)], evac)
            IP1 = []
            for j in range(8):
                IP1 += xproj_tasks(j, False)
                IP1 += g_tasks(j)
            UT = [(0, 352), (352, 352), (704, 352)]
            IP2 = []
            for j in range(8):
                def evac_b(ti, pv, c0, n, j=j):
                    act(SG1.v((c0, c0 + n)), pv, AF.Tanh, bias=vcol("hbu", j), scale=0.5)
                    if ti == 2:
                        ts("dve", SG1.v(), SG1.v(), 0.5, ALU.mult, 0.5, ALU.add)
                IP2 += inproj_tasks(UBCOL[j], HNT, UT, evac_b)

                def evac_a(ti, pv, c0, n, j=j):
                    stt("dve", V.v(j, (c0, c0 + n)), pv, vcol("b_in", 16 + j), SG1.v((c0, c0 + n)), ALU.add, ALU.mult)
                    if ti == 0:
                        fx = V.v(j, (0, 16))
                        ts("dve", fx, fx, vcol("flag"), ALU.mult)
                IP2 += inproj_tasks(UACOL[j], HNT, UT, evac_a)
            IP3 = []
            for j in range(8):
                def evac_g(ti, pv, c0, n, j=j):
                    act(YC.v(j, (c0 - HALO, c0 - HALO + n)), pv, AF.Silu, bias=vcol("b_in", 32 + j))
                IP3 += inproj_tasks(GCCOL[j], HNT, [(HALO, 512), (HALO + 512, 512)], evac_g)

            def sub_groups(t_):
                for j in range(NSPLIT):
                    if t_ < 4:
                        XP[j][0][t_]()
                    elif t_ < 8:
                        XP[j][1][t_ - 4]()
                    else:
                        XP[j][2][0]()

            prev_pos = 0
            for t_ in range(9):
                run_tasks(pro_list[prev_pos:pro_pos[t_]])
                prev_pos = pro_pos[t_]
                if t_ >= 1:
                    sub_groups(t_ - 1)
            run_tasks(pro_list[prev_pos:pro_pos[9]])
            prev_pos = pro_pos[9]
            sub_groups(8)
            setup_consts()
            lcw_v = vecs.v((VC["lcw"], VC["lcw"] + 32))
            P.op("dve", lambda e: e.tensor_tensor(
                out=dg4.v().ap, in0=ident.v().ap.unsqueeze(1).broadcast_to([128, 32, 128]),
                in1=lcw_v.ap.unsqueeze(2).broadcast_to([128, 32, 128]), op=ALU.mult),
                reads=[ident.v(), lcw_v], writes=[dg4.v()])
            rest_ip = [t for j in range(NSPLIT, 8) for t in XP[j]]
            run_tasks(interleave(pro_list[prev_pos:], rest_ip))
            dump("hnt_p", HNTP.v(), (128, 16, TP))
            P.op("act", lambda e: e.activation(out=HNT.v(None, (0, HALO)).ap, in_=HNTP.v(None, (TP - HALO, TP)).ap, func=AF.Copy),
                 reads=[HNTP.v(None, (TP - HALO, TP))], writes=[HNT.v(None, (0, HALO))])
            dump("hnt_o", HNT.v(), (128, 16, HALO + TO))
            aux_list.extend([6, 7])
            run_tasks(interleave(IP1, chain_list(True)))
            dump("hst", vecs.v((VC["hst"], VC["hst"] + 8)), (128, 8))
            chk("p0")
            run_tasks(IP2[:4])
            run_tasks(interleave(IP2[4:], chain_list(False)))
            dump("yl", YL.v(), (128, NCH, TO))
            dump("v", V.v(), (128, NCH, HALO + TO))
            chk("p1")
            built = {}

            def build_into(slot, c):
                wv = cdwb.v((c * 31, c * 31 + 31))
                P.op("dve", lambda e: e.tensor_tensor(
                    out=slot.v().ap, in0=ident.v().ap.unsqueeze(1).broadcast_to([128, 31, 128]),
                    in1=wv.ap.unsqueeze(2).broadcast_to([128, 31, 128]), op=ALU.mult),
                    reads=[ident.v(), wv], writes=[slot.v()])

            build_into(DG31X, 0)
            built[(0, 0)] = DG31X
            for j in range(8):
                run_tasks(IP3[2 * j:2 * j + 2])
                if j == 0:
                    for cc in range(8):
                        dma("pool", WOUT[cc].v(), wout_d[cc * 128:(cc + 1) * 128, :], "wout%d" % cc)
            for cc in range(8, 16):
                dma("pool", WOUT[cc].v(), wout_d[cc * 128:(cc + 1) * 128, :], "wout%d" % cc)
            for cc in range(8):
                dma("pool", PWc[cc].v(), pw_d[cc * 128:(cc + 1) * 128, :], "pw%d" % cc)

            dg_cnt = [0]
            rr = [0]
            pend_stats = [None]
            pmean, psq = PSB[6].v(), PSB[7].v()

            def dg_build(tt_, c):
                if (tt_, c) in built or c > 7:
                    return
                slot = DG31[dg_cnt[0] % 2]
                dg_cnt[0] += 1
                built[(tt_, c)] = slot
                build_into(slot, c)

            def conv_task(tt_, c):
                t0 = tt_ * 512
                dg_build(tt_, c)
                slot = built[(tt_, c)]
                pv = PSB[rr[0] % 6].v()
                rr[0] += 1
                pairs = [(slot.v(k).ap, V.v(c, (HALO + t0 + k - 30, HALO + t0 + k - 30 + 512)).ap) for k in range(31)]
                mm_group(pv, pairs, reads=[slot.v(), V.v(c, (HALO + t0 - 30, HALO + t0 + 512))])
                cb, c2b = CB[c % 2], C2B[c % 2]
                act(C.v(c), pv, AF.Identity, bias=vcol("cdb", c))
                act(cb.v(), pv, AF.Identity, bias=vcol("cdb", c))
                act(c2b.v(), pv, AF.Square, bias=vcol("cdb", c))

                def fn_m(e):
                    return e.matmul(pmean.ap, lhsT=ones.v().ap, rhs=cb.v().ap, start=(c == 0), stop=(c == 7))

                def fn_s(e):
                    return e.matmul(psq.ap, lhsT=ones.v().ap, rhs=c2b.v().ap, start=(c == 0), stop=(c == 7))
                def stats():
                    P.op("pe", fn_m, reads=[ones.v(), cb.v()], writes=[pmean])
                    P.op("pe", fn_s, reads=[ones.v(), c2b.v()], writes=[psq])
                prev = pend_stats[0]
                pend_stats[0] = stats
                if prev is not None:
                    prev()

            def flush_stats():
                if pend_stats[0] is not None:
                    pend_stats[0]()
                    pend_stats[0] = None

            def ln_stats():
                ts("dve", MU.v(), pmean, 1.0 / DC, ALU.mult)
                tt("dve", MSQ.v(), MU.v(), MU.v(), ALU.mult)
                stt("dve", MSQ.v(), psq, 1.0 / DC, MSQ.v(), ALU.mult, ALU.subtract)
                act(MSQ.v(), MSQ.v(), AF.Ln, bias=vcol("eps"))
                act(RSTD.v(), MSQ.v(), AF.Exp, scale=-0.5)

            def norm_task(c):
                eng = "dve"
                tt(eng, C.v(c), C.v(c), MU.v(), ALU.subtract)
                tt(eng, C.v(c), C.v(c), RSTD.v(), ALU.mult)
                act(S.v(c), C.v(c), AF.Silu, bias=vcol("lnb", c), scale=vcol("lnw", c))

            def pw_task(tt_, co):
                t0 = tt_ * 512
                pv = PSB[rr[0] % 6].v()
                rr[0] += 1
                pairs = [(PWc[ci].v((co * 128, (co + 1) * 128)).ap, S.v(ci).ap) for ci in range(8)]
                mm_group(pv, pairs, reads=[PWc[ci].v((co * 128, (co + 1) * 128)) for ci in range(8)] + [S.v()])
                ycv = YC.v(co, (t0, t0 + 512))
                stt("dve", ycv, pv, vcol("pwb", co), ycv, ALU.add, ALU.mult)

            for c in range(8):
                conv_task(0, c)
            dg_build(1, 0)
            dg_build(1, 1)
            flush_stats()
            dump("c", C.v(), (128, NCH, 512))
            ln_stats()
            for c in range(8):
                norm_task(c)
                if c == 7:
                    dump("s", S.v(), (128, NCH, 512))
                conv_task(1, c)
                dg_build(1, c + 2)
            flush_stats()
            for co in range(8):
                pw_task(0, co)
            ln_stats()

            dma("sp", NPOST.v(), npost_d, "c_npost")
            op_cnt = [0]

            def outproj_task(i):
                tok = i * 128
                xe, ot = XE[0], OUT[0]
                dma("sp", xe.v(), xo_d[tok:tok + 128, :], "xe0")
                base = (op_cnt[0] % 2) * 4
                op_cnt[0] += 1
                for dsl in range(4):
                    pv = PSB[base + dsl].v()
                    pairs = []
                    rd = []
                    for cc in range(16):
                        ysrc = YL if cc < 8 else YC
                        yv = ysrc.v(cc % 8, (tok, tok + 128))
                        wv = WOUT[cc].v((dsl * 512, (dsl + 1) * 512))
                        pairs.append((yv.ap, wv.ap))
                        rd += [yv, wv]
                    mm_group(pv, pairs, reads=rd)
                sb_ = VC["ssq"] + 32 + (i % 2) * 8
                sqc = vecs.v((sb_, sb_ + 4))
                for dsl in range(4):
                    act(junk.v((0, 512)), PSB[base + dsl].v(), AF.Square, accum=vecs.v((sb_ + dsl, sb_ + dsl + 1)))
                tot = vecs.v((sb_ + 4, sb_ + 5))
                sd = vecs.v((sb_ + 5, sb_ + 6))
                rs = vecs.v((sb_ + 6, sb_ + 7))
                P.op("dve", lambda e: e.tensor_reduce(out=tot.ap, in_=sqc.ap, axis=mybir.AxisListType.X, op=ALU.add),
                     reads=[sqc], writes=[tot])
                ts("dve", sd, tot, 1.0 / D, ALU.mult, EPS, ALU.add)
                act(sd, sd, AF.Sqrt)
                P.op("dve", lambda e: e.reciprocal(out=rs.ap, in_=sd.ap), reads=[sd], writes=[rs])
                for dsl in range(4):
                    sl = (dsl * 512, (dsl + 1) * 512)
                    stt("dve", ot.v(sl), PSB[base + dsl].v(), rs, NPOST.v(sl), ALU.mult, ALU.mult)
                    if dsl == 1:
                        h0 = (0, 1024)
                        tt("pool", ot.v(h0), ot.v(h0), xe.v(h0), ALU.add)
                        P.op("sp", lambda e: e.dma_start(out=out_d[tok:tok + 128, 0:1024], in_=ot.v(h0).ap),
                             reads=[ot.v(h0)], dma_key="out0a")
                h1 = (1024, 2048)
                tt("dve", ot.v(h1), ot.v(h1), xe.v(h1), ALU.add)
                P.op("sp", lambda e: e.dma_start(out=out_d[tok:tok + 128, 1024:2048], in_=ot.v(h1).ap),
                     reads=[ot.v(h1)], dma_key="out0b")

            for c in range(8):
                norm_task(c)
            for i in range(4):
                outproj_task(i)
            for co in range(8):
                pw_task(1, co)
            dump("yc", YC.v(), (128, NCH, TO))
            chk("p2")
            for i in range(4, 8):
                outproj_task(i)
        except _Stop:
            pass

        P.emit(stack)
    return nc, dbg_d


_CACHE = {}


def _layout(inp):
    f = np.float32
    x = np.asarray(inp["x"], f)
    meta = np.asarray(inp["meta_tokens"], f)
    w_in = np.asarray(inp["w_in"], f)[0]
    w_in_r = np.ascontiguousarray(w_in.reshape(16, 128, 40, 128).transpose(2, 1, 0, 3))

    def cols(v, n):
        return np.asarray(v, f).reshape(n, 128).T

    def bd(w):
        w = np.asarray(w, f)[0]
        o = np.zeros((128, 8, 128), f)
        for j in range(8):
            for a in range(2):
                o[64 * a:64 * a + 64, j, 64 * a:64 * a + 64] = w[2 * j + a]
        return o
    vec = np.zeros((128, 512), f)
    c = 0
    parts = [cols(inp["b_in"][0], 40), cols(inp["lru_conv_b"][0], 8), cols(inp["b_gate_a"][0], 8),
             cols(inp["b_gate_x"][0], 8), cols(inp["lru_lambda"][0], 8), cols(inp["conf_dw_b"][0], 8),
             cols(inp["conf_ln_w"][0], 8), cols(inp["conf_ln_b"][0], 8), cols(inp["conf_pw_b"][0], 8),
             np.asarray(inp["lru_conv_w"], f)[0].reshape(4, 8, 128).transpose(2, 1, 0).reshape(128, 32),
             np.asarray(inp["conf_dw_w"], f)[0].reshape(31, 8, 128).transpose(2, 1, 0).reshape(128, 248)]
    for p_ in parts:
        vec[:, c:c + p_.shape[1]] = p_
        c += p_.shape[1]
    shared = {
        "w_in_r": w_in_r,
        "w_out": np.ascontiguousarray(np.asarray(inp["w_out"], f)[0]),
        "pw_w": np.ascontiguousarray(np.asarray(inp["conf_pw_w"], f)[0]),
        "wga_bd": bd(inp["w_gate_a"]), "wgx_bd": bd(inp["w_gate_x"]),
        "pre_bc": np.ascontiguousarray(np.broadcast_to(np.asarray(inp["pre_norm_w"], f)[0][None, :], (128, D))),
        "post_bc": np.ascontiguousarray(np.broadcast_to(np.asarray(inp["post_norm_w"], f)[0][None, :], (128, D))),
        "vecs": vec,
    }
    maps = []
    for core in range(8):
        b, h = core // 2, core % 2
        xp = np.zeros((9 * 128, D), f)
        if h == 0:
            xp[1024:1040] = meta
        else:
            xp[0:16] = meta
            xp[16:1040] = x[b, 0:1024]
        m = dict(shared)
        m["xp"] = xp
        m["xo"] = np.ascontiguousarray(x[b, h * 1024:(h + 1) * 1024])
        m["flag"] = np.full((128, 1), float(h), f)
        maps.append(m)
    return maps


def kernel(**inputs):
    if "nc" not in _CACHE:
        _CACHE["nc"] = build_program(False)[0]
    nc = _CACHE["nc"]
    maps = _layout(inputs)
    res = run_bass_kernel_spmd(nc, maps, core_ids=list(range(8)))
    out = np.zeros((4, 2048, D), np.float32)
    for core in range(8):
        b, h = core // 2, core % 2
        out[b, h * 1024:(h + 1) * 1024] = res.results[core]["out"]
    return out
```

```python
import numpy as np
import concourse.bass as bass
import concourse.mybir as mybir
from concourse.bass_utils import run_bass_kernel_spmd

F32 = mybir.dt.float32
BF16 = mybir.dt.bfloat16
I32 = mybir.dt.int32
ALU = mybir.AluOpType
AF = mybir.ActivationFunctionType

D = 2048
DL = 1024
DC = 1024
NCH = 8
TP = 1040
TO = 1024
HALO = 32
EPS = 1e-6
SB_BYTES = 206 * 1024
DT_SIZE = {F32: 4, BF16: 2, I32: 4}


class View:
    __slots__ = ("ap", "space", "ivals")

    def __init__(self, ap, space, ivals):
        self.ap = ap
        self.space = space
        self.ivals = ivals


class Buf:
    def __init__(self, prog, space, off, dims, dt):
        self.prog = prog
        self.space = space
        self.off = off
        self.dims = tuple(dims)
        self.dt = dt
        self.esz = DT_SIZE[dt]
        n = 1
        for d in dims:
            n *= d
        self.nbytes = n * self.esz
        assert off % 4 == 0 and self.nbytes % 4 == 0, (off, dims)
        base = prog.arena[space]
        ap = base[:, off // 4:(off + self.nbytes) // 4]
        if dt != F32:
            ap = ap.bitcast(dt)
        if len(dims) == 2:
            ap = ap.rearrange("p (a b) -> p a b", a=dims[0])
        self.full = ap
        if space == "sb":
            assert off + self.nbytes <= SB_BYTES, ("sbuf overflow", off, self.nbytes)
        else:
            assert off + self.nbytes <= 16384

    def _iv(self, s, e):
        s = self.off + s * self.esz
        e = self.off + e * self.esz
        if self.space == "ps":
            s = (s // 2048) * 2048
            e = ((e + 2047) // 2048) * 2048
        return (s, e)

    def v(self, *idx, parts=None):
        dims = self.dims
        rng = []
        for i, d in enumerate(dims):
            x = idx[i] if i < len(idx) else None
            if x is None:
                rng.append((0, d, False))
            elif isinstance(x, int):
                rng.append((x, x + 1, True))
            else:
                rng.append((x[0], x[1], False))
        for (a, b, _), d in zip(rng, dims):
            assert 0 <= a < b <= d, (idx, dims)
        p = slice(None) if parts is None else slice(parts[0], parts[1])
        if len(dims) == 1:
            a, b, _ = rng[0]
            ap = self.full[p, a:b]
            iv = [self._iv(a, b)]
        else:
            (a0, b0, s0), (a1, b1, _) = rng
            if s0:
                ap = self.full[p, a0, a1:b1]
            else:
                ap = self.full[p, a0:b0, a1:b1]
            if a1 == 0 and b1 == dims[1]:
                iv = [self._iv(a0 * dims[1], b0 * dims[1])]
            else:
                iv = [self._iv(k * dims[1] + a1, k * dims[1] + b1) for k in range(a0, b0)]
        return View(ap, self.space, iv)


class Op:
    __slots__ = ("eng", "fn", "waits", "done", "clock", "idx", "is_dma")


class Prog:
    ENGS = ("pe", "act", "dve", "pool", "sp")

    def __init__(self, nc, sb, ps):
        self.nc = nc
        self.arena = {"sb": sb, "ps": ps}
        self.ops = []
        self.streams = {e: [] for e in self.ENGS}
        self.eng_count = {e: 0 for e in self.ENGS}
        self.clock = {e: {} for e in self.ENGS}
        self.dma_count = {}
        self.recs = {"sb": [], "ps": [], "dram": []}
        self.sb_off = 0

    def alloc(self, dims, dt, off=None, space="sb"):
        if off is None:
            off = self.sb_off
            b = Buf(self, space, off, dims, dt)
            self.sb_off = off + ((b.nbytes + 31) // 32) * 32
            return b
        return Buf(self, space, off, dims, dt)

    def _deps(self, reads, writes):
        deps = set()
        for vw in reads:
            recs = self.recs[vw.space]
            for (s, e) in vw.ivals:
                for r in recs:
                    if r[3] and r[0] < e and s < r[1]:
                        deps.add(r[2])
        for vw in writes:
            recs = self.recs[vw.space]
            for (s, e) in vw.ivals:
                for r in recs:
                    if r[0] < e and s < r[1]:
                        deps.add(r[2])
        return deps

    def _record(self, op_id, eng, reads, writes):
        for vw in writes:
            for (s, e) in vw.ivals:
                recs = self.recs[vw.space]
                recs[:] = [r for r in recs if not (s <= r[0] and r[1] <= e)]
                recs.append((s, e, op_id, True, eng))
        for vw in reads:
            for (s, e) in vw.ivals:
                recs = self.recs[vw.space]
                recs[:] = [r for r in recs if not (r[0] == s and r[1] == e and (not r[3]) and r[4] == eng)]
                recs.append((s, e, op_id, False, eng))

    def op(self, eng, fn, reads=(), writes=(), dma_key=None, extra_deps=()):
        reads = [r for r in reads if r is not None]
        writes = [w for w in writes if w is not None]
        writes = writes + [r for r in reads if r.space == "ps"]
        reads = [r for r in reads if r.space != "ps"]
        deps = self._deps(reads, writes)
        deps.update(extra_deps)
        o = Op()
        o.eng = eng
        o.fn = fn
        o.is_dma = dma_key is not None
        clk = self.clock[eng]
        waits = []
        for d in sorted(deps):
            dop = self.ops[d]
            k, v = dop.done
            if clk.get(k, 0) >= v:
                continue
            waits.append((k, v))
            for kk, vv in dop.clock.items():
                if clk.get(kk, 0) < vv:
                    clk[kk] = vv
        best = {}
        for k, v in waits:
            best[k] = max(best.get(k, 0), v)
        o.waits = sorted(best.items())
        if dma_key is not None:
            n = self.dma_count.get(dma_key, 0) + 1
            self.dma_count[dma_key] = n
            o.done = ("dma:" + dma_key, 16 * n)
        else:
            self.eng_count[eng] += 1
            o.done = (eng, self.eng_count[eng])
        o.clock = dict(clk)
        o.clock[o.done[0]] = o.done[1]
        o.idx = len(self.ops)
        self.ops.append(o)
        self.streams[eng].append(o)
        self._record(o.idx, eng, reads, writes)
        return o.idx

    def emit(self, stack):
        nc = self.nc
        sems = {}
        keys = list(self.ENGS[:4]) + sorted("dma:" + k for k in self.dma_count)
        for k in keys:
            sems[k] = stack.enter_context(nc.semaphore("s_" + k.replace(":", "_")))
        block = stack.enter_context(nc.Block())

        def run(engine_obj, stream, final_waits=()):
            for o in stream:
                for (k, v) in o.waits:
                    engine_obj.wait_ge(sems[k], v)
                ins = o.fn(engine_obj)
                ins.then_inc(sems[o.done[0]], 16 if o.is_dma else 1)
            for (k, v) in final_waits:
                engine_obj.wait_ge(sems[k], v)

        final = [("dma:" + k, 16 * n) for k, n in self.dma_count.items()]

        @block.tensor
        def _(e):
            run(e, self.streams["pe"])

        @block.scalar
        def _(e):
            run(e, self.streams["act"])

        @block.vector
        def _(e):
            run(e, self.streams["dve"])

        @block.gpsimd
        def _(e):
            run(e, self.streams["pool"])

        @block.sync
        def _(e):
            run(e, self.streams["sp"], final)


def interleave(*lists):
    lists = [l for l in lists if l]
    out = []
    if not lists:
        return out
    n = max(len(l) for l in lists)
    pos = [0] * len(lists)
    for step in range(1, n + 1):
        for i, l in enumerate(lists):
            tgt = (step * len(l) + n - 1) // n
            while pos[i] < tgt:
                out.append(l[pos[i]])
                pos[i] += 1
    return out


class _Stop(Exception):
    pass


def build_program(dbg=False, stop=None):
    import contextlib
    nc = bass.Bass("TRN2", target_bir_lowering=False)
    dram = {}

    def din(name, shape):
        dram[name] = nc.dram_tensor(name, list(shape), F32, kind="ExternalInput").ap()
        return dram[name]

    xp_d = din("xp", (9 * 128, D))
    xo_d = din("xo", (TO, D))
    flag_d = din("flag", (128, 1))
    win_d = din("w_in_r", (40, 128, 16, 128))
    wout_d = din("w_out", (D, D))
    pw_d = din("pw_w", (DC, DC))
    wga_d = din("wga_bd", (128, NCH, 128))
    wgx_d = din("wgx_bd", (128, NCH, 128))
    npre_d = din("pre_bc", (128, D))
    npost_d = din("post_bc", (128, D))
    vec_d = din("vecs", (128, 512))
    out_d = nc.dram_tensor("out", [TO, D], F32, kind="ExternalOutput").ap()
    dbg_d = {}

    VC = {}
    c = 0
    for nm, w in (("b_in", 40), ("lcb", 8), ("bga", 8), ("bgx", 8), ("lam", 8), ("cdb", 8),
                  ("lnw", 8), ("lnb", 8), ("pwb", 8), ("lcw", 32), ("cdw", 248)):
        VC[nm] = c
        c += w
    VC["cexp"] = c; c += 8
    VC["c2"] = c; c += 8
    VC["ch"] = c; c += 8
    VC["hga"] = c; c += 8
    VC["hgx"] = c; c += 8
    VC["hbu"] = c; c += 8
    VC["qtr"] = c; c += 1
    VC["tmp"] = c; c += 8
    VC["eps"] = c; c += 1
    VC["one"] = c; c += 1
    VC["flag"] = c; c += 1
    VC["hst"] = c; c += 8
    VC["hin"] = c; c += 8
    VC["ssq"] = c; c += 48
    assert c <= 512

    stack = contextlib.ExitStack()
    with stack:
        sb = stack.enter_context(nc.sbuf_tensor("arena", [128, SB_BYTES // 4], F32))
        ps = stack.enter_context(nc.psum_tensor("psum", [128, 4096], F32))
        P = Prog(nc, sb, ps)

        vecs = P.alloc((512,), F32)
        ident = P.alloc((128,), BF16)
        ones = P.alloc((128,), BF16)
        wga = P.alloc((NCH, 128), BF16)
        wgx = P.alloc((NCH, 128), BF16)
        cdwb = P.alloc((256,), BF16)
        iota_i = P.alloc((128,), I32)
        iota_f = P.alloc((128,), F32)
        CB = [P.alloc((512,), BF16) for _ in range(2)]
        C2B = [P.alloc((512,), BF16) for _ in range(2)]
        yl0 = P.sb_off
        YL = P.alloc((NCH, TO), BF16)
        YC = P.alloc((NCH, TO), BF16)
        P.alloc((256,), BF16)
        HNTP = P.alloc((16, TP), BF16, off=yl0)
        assert HNTP.nbytes <= P.sb_off - yl0
        v0 = P.sb_off
        V = P.alloc((NCH, HALO + TO), BF16)
        XLP = P.alloc((NCH, TP + 4), BF16, off=v0)
        assert XLP.nbytes <= V.nbytes
        XLO = P.alloc((NCH, TO + 4), BF16, off=yl0 + 16384)
        assert yl0 + 16384 + XLO.nbytes <= v0
        dg4 = P.alloc((NCH * 4, 128), BF16)
        g0 = P.sb_off
        SG1 = P.alloc((HALO + TO,), F32)
        MU = P.alloc((512,), F32)
        RSTD = P.alloc((512,), F32)
        MSQ = P.alloc((512,), F32)
        NPRE = P.alloc((D,), F32, off=g0)
        DG31X = P.alloc((31, 128), BF16, off=g0)
        assert DG31X.nbytes <= 4224 + 3 * 2048
        junk = CB[0]
        y0 = P.sb_off
        NSLOT = 5
        HNT = P.alloc((16, HALO + TO), BF16, off=y0)
        WIN = [P.alloc((16, 128), BF16, off=y0 + HNT.nbytes + i * 4096) for i in range(NSLOT)]
        ysz = HNT.nbytes + NSLOT * 4096
        C = P.alloc((NCH, 512), F32, off=y0)
        S = P.alloc((NCH, 512), BF16, off=y0 + C.nbytes)
        DG31 = [P.alloc((31, 128), BF16, off=y0 + C.nbytes + S.nbytes + i * 7936) for i in range(2)]
        pwy = y0 + C.nbytes + S.nbytes + 2 * 7936
        assert pwy + 4 * 2048 <= y0 + ysz
        XE = [P.alloc((D,), F32, off=v0)]
        OUT = [P.alloc((D,), F32, off=v0 + 8192)]
        NPOST = P.alloc((D,), F32, off=y0 + C.nbytes + S.nbytes)
        x0 = y0 + ysz
        XS = [P.alloc((D,), F32, off=x0 + i * 8192) for i in range(2)]
        HNB = [P.alloc((D,), BF16, off=x0 + 16384 + i * 4096) for i in range(2)]
        JK = P.alloc((D,), BF16, off=x0 + 24576)
        lo = x0 + 28672
        TL = TP
        Rb = [P.alloc((TL,), F32, off=lo + (0 + i) * 4160) for i in range(2)]
        Ib = [P.alloc((TL,), F32, off=lo + (2 + i) * 4160) for i in range(2)]
        Ab = [P.alloc((TL,), F32, off=lo + (4 + i) * 4160) for i in range(2)]
        Hb = [P.alloc((TL,), F32, off=lo + (6 + i) * 4160) for i in range(2)]
        XC = [P.alloc((TL,), F32, off=lo + (8 + i) * 4160) for i in range(2)]
        XCB = [P.alloc((TL,), BF16, off=lo + 10 * 4160 + i * 2112) for i in range(2)]
        XS += [P.alloc((D,), F32, off=lo + i * 8192) for i in range(2)]
        HNB += [P.alloc((D,), BF16, off=lo + 16384)]
        xsz = 28672 + 10 * 4160 + 2 * 2112
        WOUT = [P.alloc((D,), BF16, off=x0 + i * 4096) for i in range(16)]
        assert xsz >= 16 * 4096 + 4 * 2048
        PWc = [P.alloc((DC,), BF16, off=x0 + 65536 + i * 2048) for i in range(4)] + \
              [P.alloc((DC,), BF16, off=pwy + i * 2048) for i in range(4)]
        assert x0 + xsz <= SB_BYTES, (x0 + xsz, SB_BYTES)
        PSB = [P.alloc((512,), F32, off=b * 2048, space="ps") for b in range(8)]
        PST = P.alloc((16, 128), BF16, off=6 * 2048, space="ps")

        def vcol(nm, j=0, n=1):
            return vecs.v((VC[nm] + j, VC[nm] + j + n))

        def dma(eng, out_v, in_ap, key, reads=()):
            return P.op(eng, lambda e: e.dma_start(out=out_v.ap, in_=in_ap), reads=reads, writes=[out_v], dma_key=key)

        def dump(name, view, shape):
            if not dbg:
                return
            if name not in dbg_d:
                dbg_d[name] = nc.dram_tensor("dbg_" + name, list(shape), view.ap.dtype, kind="ExternalOutput").ap()
            P.op("sp", lambda e: e.dma_start(out=dbg_d[name], in_=view.ap), reads=[view], dma_key="out_dbg_" + name)

        def act(out_v, in_v, func, bias=None, scale=None, accum=None, extra_r=()):
            kw = {}
            rd = [in_v] + list(extra_r)
            if bias is not None:
                if isinstance(bias, View):
                    kw["bias"] = bias.ap
                    rd.append(bias)
                else:
                    kw["bias"] = bias
            if scale is not None:
                if isinstance(scale, View):
                    kw["scale"] = scale.ap
                    rd.append(scale)
                else:
                    kw["scale"] = scale
            wr = [out_v]
            if accum is not None:
                kw["accum_out"] = accum.ap
                wr.append(accum)
            return P.op("act", lambda e: e.activation(out=out_v.ap, in_=in_v.ap, func=func, **kw), reads=rd, writes=wr)

        def tt(eng, out_v, a, b, op):
            return P.op(eng, lambda e: e.tensor_tensor(out=out_v.ap, in0=a.ap, in1=b.ap, op=op), reads=[a, b], writes=[out_v])

        def ts(eng, out_v, a, s1, op0, s2=None, op1=None):
            rd = [a]
            s1a = s1.ap if isinstance(s1, View) else s1
            s2a = s2.ap if isinstance(s2, View) else s2
            if isinstance(s1, View):
                rd.append(s1)
            if isinstance(s2, View):
                rd.append(s2)
            if op1 is None:
                return P.op(eng, lambda e: e.tensor_scalar(out=out_v.ap, in0=a.ap, scalar1=s1a, scalar2=None, op0=op0), reads=rd, writes=[out_v])
            return P.op(eng, lambda e: e.tensor_scalar(out=out_v.ap, in0=a.ap, scalar1=s1a, scalar2=s2a, op0=op0, op1=op1), reads=rd, writes=[out_v])

        def stt(eng, out_v, a, s, b, op0, op1):
            rd = [a, b]
            sa = s.ap if isinstance(s, View) else s
            if isinstance(s, View):
                rd.append(s)
            return P.op(eng, lambda e: e.scalar_tensor_tensor(out=out_v.ap, in0=a.ap, scalar=sa, in1=b.ap, op0=op0, op1=op1), reads=rd, writes=[out_v])

        def mm_group(out_v, pairs, reads):
            def fn(e):
                ins = None
                n = len(pairs)
                for i, (l, r) in enumerate(pairs):
                    ins = e.matmul(out_v.ap, lhsT=l, rhs=r, start=(i == 0), stop=(i == n - 1))
                return ins
            return P.op("pe", fn, reads=reads, writes=[out_v])

        NROW = XS[3].v(parts=(0, 1))
        dma("sp", NROW, npre_d[0:1, :], "c_npre")
        dma("sp", XS[0].v(), xp_d[0:128, :], "xs0")
        dma("sp", vecs.v((0, VC["cexp"])), vec_d[:, 0:VC["cexp"]], "c_vecs")
        dma("sp", vecs.v((VC["flag"], VC["flag"] + 1)), flag_d, "c_flag")
        ONESF = P.alloc((128,), F32, off=JK.off)
        P.op("dve", lambda e: e.memset(ONESF.v().ap, 1.0), writes=[ONESF.v()])
        for b_ in range(4):
            pv_ = PSB[b_].v()
            rhs_ = View(NROW.ap[:, b_ * 512:(b_ + 1) * 512], NROW.space, NROW.ivals)
            P.op("pe", lambda e, pv_=pv_, rhs_=rhs_: e.matmul(pv_.ap, lhsT=ONESF.v(parts=(0, 1)).ap, rhs=rhs_.ap, start=True, stop=True),
                 reads=[ONESF.v(), rhs_], writes=[pv_])
            nv_ = NPRE.v((b_ * 512, (b_ + 1) * 512))
            P.op("dve", lambda e, nv_=nv_, pv_=pv_: e.tensor_copy(out=nv_.ap, in_=pv_.ap), reads=[pv_], writes=[nv_])
        dma("pool", wga.v(), wga_d, "c_wga")
        dma("pool", wgx.v(), wgx_d, "c_wgx")
        P.op("pool", lambda e: e.iota(iota_i.v().ap, pattern=[[1, 128]], base=0, channel_multiplier=-1), writes=[iota_i.v()])
        P.op("dve", lambda e: e.tensor_copy(out=iota_f.v().ap, in_=iota_i.v().ap), reads=[iota_i.v()], writes=[iota_f.v()])
        P.op("dve", lambda e: e.tensor_single_scalar(out=ident.v().ap, in_=iota_f.v().ap, scalar=0.0, op=ALU.is_equal),
             reads=[iota_f.v()], writes=[ident.v()])
        P.op("dve", lambda e: e.memset(ones.v().ap, 1.0), writes=[ones.v()])
        P.op("dve", lambda e: e.memset(vcol("eps").ap, EPS), writes=[vcol("eps")])
        P.op("dve", lambda e: e.memset(vcol("one").ap, 1.0), writes=[vcol("one")])
        P.op("dve", lambda e: e.memset(XLP.v(None, (0, 4)).ap, 0.0), writes=[XLP.v(None, (0, 4))])
        act(vcol("tmp", 0, 8), vcol("lam", 0, 8), AF.Exp, scale=-1.0)
        act(vcol("tmp", 0, 8), vcol("tmp", 0, 8), AF.Ln, bias=vcol("one"))
        ts("dve", vcol("cexp", 0, 8), vcol("tmp", 0, 8), -8.0, ALU.mult)
        ts("dve", vcol("c2", 0, 8), vcol("tmp", 0, 8), -16.0, ALU.mult)
        ts("dve", vcol("ch", 0, 8), vcol("tmp", 0, 8), -4.0, ALU.mult)
        ts("dve", vcol("hga", 0, 8), vcol("bga", 0, 8), 0.5, ALU.mult)
        ts("dve", vcol("hgx", 0, 8), vcol("bgx", 0, 8), 0.5, ALU.mult)
        ts("dve", vcol("hbu", 0, 8), vcol("b_in", 24, 8), 0.5, ALU.mult)
        P.op("dve", lambda e: e.memset(vcol("qtr").ap, 0.25), writes=[vcol("qtr")])
        cdw_f = vecs.v((VC["cdw"], VC["cdw"] + 248))
        P.op("dve", lambda e: e.tensor_copy(out=cdwb.v((0, 248)).ap, in_=cdw_f.ap), reads=[cdw_f], writes=[cdwb.v((0, 248))])
        win_seq = []
        state = {"next_load": 0}

        def win_issue_loads(upto):
            while state["next_load"] < min(upto, len(win_seq)):
                q = state["next_load"]
                sl = q % NSLOT
                dma("pool", WIN[sl].v(), win_d[win_seq[q]], "win%d" % sl)
                state["next_load"] += 1

        pro_cnt = [0]

        def prologue_tasks(src_ap, dst, dst_col, ncols):
            i = pro_cnt[0]
            pro_cnt[0] += 1
            xs = XS[i % 4]
            hb = HNB[i % 3]
            sq = vcol("ssq", (i % 8) * 4, 1)
            sd = vcol("ssq", (i % 8) * 4 + 1, 1)
            rs = vcol("ssq", (i % 8) * 4 + 2, 1)

            def s0():
                if i > 0:
                    dma("sp", xs.v(), src_ap, "xs%d" % (i % 4))

            def s1():
                act(JK.v(), xs.v(), AF.Square, accum=sq)

            def s2():
                ts("dve", sd, sq, 1.0 / D, ALU.mult, EPS, ALU.add)
                act(sd, sd, AF.Sqrt)

            def s2b():
                P.op("dve", lambda e: e.reciprocal(out=rs.ap, in_=sd.ap), reads=[sd], writes=[rs])
                stt("dve", hb.v(), xs.v(), rs, NPRE.v(), ALU.mult, ALU.mult)

            def s3():
                def fn(e):
                    ins = None
                    for k in range(16):
                        ins = e.transpose(out=PST.v(k).ap, in_=hb.v((k * 128, (k + 1) * 128)).ap, identity=ident.v().ap)
                    return ins
                P.op("pe", fn, reads=[hb.v(), ident.v()], writes=[PST.v()])
                for (k0, k1, eng) in ((0, 8, "act"), (8, 16, "dve")):
                    src = PST.v((k0, k1), (0, ncols))
                    dv = dst.v((k0, k1), (dst_col, dst_col + ncols))
                    if eng == "act":
                        P.op("act", lambda e, src=src, dv=dv: e.activation(out=dv.ap, in_=src.ap, func=AF.Copy), reads=[src], writes=[dv])
                    else:
                        P.op("dve", lambda e, src=src, dv=dv: e.tensor_copy(out=dv.ap, in_=src.ap), reads=[src], writes=[dv])
            return [s0, s1, s2, s2b, s3]

        def pipeline_prologue(tiles):
            out, pos = [], []
            n = len(tiles)
            for step in range(-1, n + 3):
                if 0 <= step + 1 < n:
                    out.append(tiles[step + 1][0])
                if 0 <= step < n:
                    out.append(tiles[step][1])
                if 0 <= step - 1 < n:
                    out.append(tiles[step - 1][2])
                if 0 <= step - 2 < n:
                    out.append(tiles[step - 2][3])
                if 0 <= step - 3 < n:
                    out.append(tiles[step - 3][4])
                    pos.append(len(out))
            return out, pos

        bank_rr = [0]
        aux_rr = [0]
        NSPLIT = 4
        aux_list = [4, 5]

        def aux_bank():
            b = aux_list[aux_rr[0] % len(aux_list)]
            aux_rr[0] += 1
            return PSB[b]

        def inproj_tasks(col, src, tiles, evac):
            q = len(win_seq)
            win_seq.append(col)
            wslot = WIN[q % NSLOT]
            tasks = []
            for ti, (c0, n) in enumerate(tiles):
                def task(ti=ti, c0=c0, n=n):
                    if q == 0 and ti == 0:
                        win_issue_loads(NSLOT)
                    if n >= 128:
                        pb = PSB[bank_rr[0] % 4]
                        bank_rr[0] += 1
                    else:
                        pb = aux_bank()
                    pv = pb.v((0, n))
                    pairs = [(wslot.v(k).ap, src.v(k, (c0, c0 + n)).ap) for k in range(16)]
                    mm_group(pv, pairs, reads=[wslot.v(), src.v(None, (c0, c0 + n))])
                    evac(ti, pv, c0, n)
                    if ti == len(tiles) - 1:
                        win_issue_loads(q + NSLOT + 1)
                tasks.append(task)
            return tasks

        def inproj_split_tasks(col, src, evac):
            q = len(win_seq)
            win_seq.append(col)
            wslot = WIN[q % NSLOT]
            banks = {}

            def sub(tile, c0, s_):
                def task():
                    if q == 0 and tile == 0 and s_ == 0:
                        win_issue_loads(NSPLIT)
                    if q == 0 and tile == 1 and s_ == 0:
                        win_issue_loads(NSLOT)
                    if s_ == 0:
                        banks[tile] = PSB[bank_rr[0] % 4]
                        bank_rr[0] += 1
                    pb = banks[tile]
                    lo_, hi_ = c0 + 128 * s_, c0 + 128 * s_ + 128
                    pv = pb.v((128 * s_, 128 * s_ + 128))
                    pairs = [(wslot.v(k).ap, src.v(k, (lo_, hi_)).ap) for k in range(16)]
                    mm_group(pv, pairs, reads=[wslot.v(), src.v(None, (lo_, hi_))])
                    if s_ == 3:
                        evac(tile, pb.v((0, 512)), c0, 512)
                return task

            def task_c():
                pv = aux_bank().v((0, 16))
                pairs = [(wslot.v(k).ap, src.v(k, (1024, 1040)).ap) for k in range(16)]
                mm_group(pv, pairs, reads=[wslot.v(), src.v(None, (1024, 1040))])
                evac(2, pv, 1024, 16)
                win_issue_loads(q + NSLOT + 1)
            return [sub(0, 0, s_) for s_ in range(4)], [sub(1, 512, s_) for s_ in range(4)], [task_c]

        def xproj_tasks(j, prefix):
            bcol = vcol("b_in", j)
            if prefix:
                def evac_x(ti, pv, c0, n):
                    act(XLP.v(j, (3 + c0, 3 + c0 + n)), pv, AF.Identity, bias=bcol)
                    if ti == 1:
                        fx = XLP.v(j, (3 + 1021, 3 + 1024))
                        ts("dve", fx, fx, vcol("flag"), ALU.mult)
                if j < NSPLIT:
                    return inproj_split_tasks(j, HNTP, evac_x)
                return inproj_tasks(j, HNTP, [(0, 512), (512, 512), (1024, 16)], evac_x)

            def evac_x(ti, pv, c0, n):
                if ti == 0:
                    srcv = XLP.v(j, (3 + TP - 3, 3 + TP))
                    P.op("act", lambda e: e.activation(out=XLO.v(j, (0, 3)).ap, in_=srcv.ap, func=AF.Copy), reads=[srcv], writes=[XLO.v(j, (0, 3))])
                ts("dve", XLO.v(j, (3 + c0 - 32, 3 + c0 - 32 + n)), pv, bcol, ALU.add)
            return inproj_tasks(j, HNT, [(32, 512), (544, 512)], evac_x)

        def lru_chain(j, prefix):
            T = TP if prefix else TO
            p = j % 2
            XLb = XLP if prefix else XLO
            Rj, Ij, Aj, Hj, XCj, XCBj = Rb[p], Ib[p], Ab[p], Hb[p], XC[p], XCB[p]
            ttiles = [(0, 512), (512, 512), (1024, 16)] if prefix else [(0, 512), (512, 512)]
            CV, GT = [], []
            for (t0, n) in ttiles:
                def btask(t0=t0, n=n):
                    pv = aux_bank().v((0, n))
                    pairs = [(dg4.v(j * 4 + k).ap, XLb.v(j, (t0 + k, t0 + k + n)).ap) for k in range(4)]
                    mm_group(pv, pairs, reads=[dg4.v((j * 4, j * 4 + 4)), XLb.v(j, (t0, t0 + n + 3))])
                    ts("dve", XCj.v((t0, t0 + n)), pv, vcol("lcb", j), ALU.add)
                    act(XCBj.v((t0, t0 + n)), pv, AF.Identity, bias=vcol("lcb", j))
                CV.append(btask)
            for (t0, n) in ttiles:
                def ctask_r(t0=t0, n=n):
                    pv = aux_bank().v((0, n))
                    mm_group(pv, [(wga.v(j).ap, XCBj.v((t0, t0 + n)).ap)], reads=[wga.v(j), XCBj.v((t0, t0 + n))])
                    act(Rj.v((t0, t0 + n)), pv, AF.Tanh, bias=vcol("hga", j), scale=0.5)

                def ctask_i(t0=t0, n=n):
                    pv = aux_bank().v((0, n))
                    mm_group(pv, [(wgx.v(j).ap, XCBj.v((t0, t0 + n)).ap)], reads=[wgx.v(j), XCBj.v((t0, t0 + n))])
                    act(Ij.v((t0, t0 + n)), pv, AF.Tanh, bias=vcol("hgx", j), scale=0.5)
                    stt("dve", Ij.v((t0, t0 + n)), Ij.v((t0, t0 + n)), 1.0, XCj.v((t0, t0 + n)), ALU.add, ALU.mult)
                GT.append(ctask_r)
                GT.append(ctask_i)

            r = Rj.v((0, T)); i_ = Ij.v((0, T)); a = Aj.v((0, T)); h = Hj.v((0, T))

            def ch1():
                act(a, r, AF.Exp, scale=vcol("ch", j), bias=vcol("ch", j))
                act(r, r, AF.Exp, scale=vcol("cexp", j), bias=vcol("cexp", j))

            def ch2():
                act(r, r, AF.Sqrt, bias=vcol("qtr"), scale=-0.25)
                tt("dve", i_, i_, r, ALU.mult)

            def ch3():
                if prefix:
                    P.op("dve", lambda e: e.tensor_tensor_scan(out=Hj.v((0, 1024)).ap, data0=Aj.v((0, 1024)).ap, data1=Ij.v((0, 1024)).ap,
                                                             initial=0.0, op0=ALU.mult, op1=ALU.add),
                         reads=[Aj.v((0, 1024)), Ij.v((0, 1024))], writes=[Hj.v((0, 1024))])
                    ts("dve", vcol("hin", j), Hj.v((1023, 1024)), vcol("flag"), ALU.mult)
                    P.op("dve", lambda e: e.tensor_tensor_scan(out=Hj.v((1024, TP)).ap, data0=Aj.v((1024, TP)).ap, data1=Ij.v((1024, TP)).ap,
                                                             initial=vcol("hin", j).ap, op0=ALU.mult, op1=ALU.add),
                         reads=[Aj.v((1024, TP)), Ij.v((1024, TP)), vcol("hin", j)], writes=[Hj.v((1024, TP))])
                    P.op("dve", lambda e: e.tensor_copy(out=vcol("hst", j).ap, in_=Hj.v((TP - 1, TP)).ap),
                         reads=[Hj.v((TP - 1, TP))], writes=[vcol("hst", j)])
                else:
                    P.op("dve", lambda e: e.tensor_tensor_scan(out=h.ap, data0=a.ap, data1=i_.ap,
                                                             initial=vcol("hst", j).ap, op0=ALU.mult, op1=ALU.add),
                         reads=[a, i_, vcol("hst", j)], writes=[h])
                    tt("dve", YL.v(j), h, YL.v(j), ALU.mult)
            return CV, GT, [ch1, ch2, ch3]

        def chain_list(prefix):
            parts = [lru_chain(j, prefix) for j in range(8)]
            out = list(parts[0][0])
            for s_ in range(9):
                gt = parts[s_][1] if s_ < 8 else []
                cv = parts[s_ + 1][0] if s_ + 1 < 8 else []
                ch = parts[s_ - 1][2] if s_ >= 1 else []
                out += interleave(gt + cv, ch)
            return out

        def run_tasks(ts_):
            for t in ts_:
                t()

        def chk(name):
            if stop == name:
                raise _Stop()

        XCOL = list(range(0, 8))
        GLCOL = list(range(8, 16))
        UACOL = list(range(16, 24))
        UBCOL = list(range(24, 32))
        GCCOL = list(range(32, 40))

        try:
            pre_tiles = [prologue_tasks(xp_d[i * 128:(i + 1) * 128, :], HNTP, i * 128, 128 if i < 8 else 16) for i in range(9)]
            own_tiles = [prologue_tasks(xo_d[i * 128:(i + 1) * 128, :], HNT, HALO + i * 128, 128) for i in range(8)]
            pro_list, pro_pos = pipeline_prologue(pre_tiles + own_tiles)
            XP = [xproj_tasks(j, True) for j in range(8)]

            def g_tasks(j):
                def evac(ti, pv, c0, n):
                    act(YL.v(j, (c0 - HALO, c0 - HALO + n)), pv, AF.Silu, bias=vcol("b_in", 8 + j))
                return inproj_tasks(GLCOL[j], HNT, [(HALO, 512), (HALO + 512, 512)], evac)
            IP1 = []
            for j in range(8):
                IP1 += xproj_tasks(j, False)
                IP1 += g_tasks(j)
            UT = [(0, 352), (352, 352), (704, 352)]
            IP2 = []
            for j in range(8):
                def evac_b(ti, pv, c0, n, j=j):
                    act(SG1.v((c0, c0 + n)), pv, AF.Tanh, bias=vcol("hbu", j), scale=0.5)
                    if ti == 2:
                        ts("dve", SG1.v(), SG1.v(), 0.5, ALU.mult, 0.5, ALU.add)
                IP2 += inproj_tasks(UBCOL[j], HNT, UT, evac_b)

                def evac_a(ti, pv, c0, n, j=j):
                    stt("dve", V.v(j, (c0, c0 + n)), pv, vcol("b_in", 16 + j), SG1.v((c0, c0 + n)), ALU.add, ALU.mult)
                    if ti == 0:
                        fx = V.v(j, (0, 16))
                        ts("dve", fx, fx, vcol("flag"), ALU.mult)
                IP2 += inproj_tasks(UACOL[j], HNT, UT, evac_a)
            IP3 = []
            for j in range(8):
                def evac_g(ti, pv, c0, n, j=j):
                    act(YC.v(j, (c0 - HALO, c0 - HALO + n)), pv, AF.Silu, bias=vcol("b_in", 32 + j))
                IP3 += inproj_tasks(GCCOL[j], HNT, [(HALO, 512), (HALO + 512, 512)], evac_g)

            def sub_groups(t_):
                for j in range(NSPLIT):
                    if t_ < 4:
                        XP[j][0][t_]()
                    elif t_ < 8:
                        XP[j][1][t_ - 4]()
                    else:
                        XP[j][2][0]()

            prev_pos = 0
            for t_ in range(9):
                run_tasks(pro_list[prev_pos:pro_pos[t_]])
                prev_pos = pro_pos[t_]
                if t_ >= 1:
                    sub_groups(t_ - 1)
            run_tasks(pro_list[prev_pos:pro_pos[9]])
            prev_pos = pro_pos[9]
            sub_groups(8)
            lcw_v = vecs.v((VC["lcw"], VC["lcw"] + 32))
            P.op("dve", lambda e: e.tensor_tensor(
                out=dg4.v().ap, in0=ident.v().ap.unsqueeze(1).broadcast_to([128, 32, 128]),
                in1=lcw_v.ap.unsqueeze(2).broadcast_to([128, 32, 128]), op=ALU.mult),
                reads=[ident.v(), lcw_v], writes=[dg4.v()])
            rest_ip = [t for j in range(NSPLIT, 8) for t in XP[j]]
            run_tasks(interleave(pro_list[prev_pos:], rest_ip))
            dump("hnt_p", HNTP.v(), (128, 16, TP))
            P.op("act", lambda e: e.activation(out=HNT.v(None, (0, HALO)).ap, in_=HNTP.v(None, (TP - HALO, TP)).ap, func=AF.Copy),
                 reads=[HNTP.v(None, (TP - HALO, TP))], writes=[HNT.v(None, (0, HALO))])
            dump("hnt_o", HNT.v(), (128, 16, HALO + TO))
            aux_list.extend([6, 7])
            run_tasks(interleave(IP1, chain_list(True)))
            dump("hst", vecs.v((VC["hst"], VC["hst"] + 8)), (128, 8))
            chk("p0")
            run_tasks(IP2[:4])
            run_tasks(interleave(IP2[4:], chain_list(False)))
            dump("yl", YL.v(), (128, NCH, TO))
            dump("v", V.v(), (128, NCH, HALO + TO))
            chk("p1")
            built = {}

            def build_into(slot, c):
                wv = cdwb.v((c * 31, c * 31 + 31))
                P.op("dve", lambda e: e.tensor_tensor(
                    out=slot.v().ap, in0=ident.v().ap.unsqueeze(1).broadcast_to([128, 31, 128]),
                    in1=wv.ap.unsqueeze(2).broadcast_to([128, 31, 128]), op=ALU.mult),
                    reads=[ident.v(), wv], writes=[slot.v()])

            build_into(DG31X, 0)
            built[(0, 0)] = DG31X
            for j in range(8):
                run_tasks(IP3[2 * j:2 * j + 2])
                if j == 0:
                    for cc in range(8):
                        dma("pool", WOUT[cc].v(), wout_d[cc * 128:(cc + 1) * 128, :], "wout%d" % cc)
            for cc in range(8, 16):
                dma("pool", WOUT[cc].v(), wout_d[cc * 128:(cc + 1) * 128, :], "wout%d" % cc)
            for cc in range(8):
                dma("pool", PWc[cc].v(), pw_d[cc * 128:(cc + 1) * 128, :], "pw%d" % cc)

            dg_cnt = [0]
            rr = [0]
            pend_stats = [None]
            pmean, psq = PSB[6].v(), PSB[7].v()

            def dg_build(tt_, c):
                if (tt_, c) in built or c > 7:
                    return
                slot = DG31[dg_cnt[0] % 2]
                dg_cnt[0] += 1
                built[(tt_, c)] = slot
                build_into(slot, c)

            def conv_task(tt_, c):
                t0 = tt_ * 512
                dg_build(tt_, c)
                slot = built[(tt_, c)]
                pv = PSB[rr[0] % 6].v()
                rr[0] += 1
                pairs = [(slot.v(k).ap, V.v(c, (HALO + t0 + k - 30, HALO + t0 + k - 30 + 512)).ap) for k in range(31)]
                mm_group(pv, pairs, reads=[slot.v(), V.v(c, (HALO + t0 - 30, HALO + t0 + 512))])
                cb, c2b = CB[c % 2], C2B[c % 2]
                act(C.v(c), pv, AF.Identity, bias=vcol("cdb", c))
                act(cb.v(), pv, AF.Identity, bias=vcol("cdb", c))
                act(c2b.v(), pv, AF.Square, bias=vcol("cdb", c))

                def fn_m(e):
                    return e.matmul(pmean.ap, lhsT=ones.v().ap, rhs=cb.v().ap, start=(c == 0), stop=(c == 7))

                def fn_s(e):
                    return e.matmul(psq.ap, lhsT=ones.v().ap, rhs=c2b.v().ap, start=(c == 0), stop=(c == 7))
                def stats():
                    P.op("pe", fn_m, reads=[ones.v(), cb.v()], writes=[pmean])
                    P.op("pe", fn_s, reads=[ones.v(), c2b.v()], writes=[psq])
                prev = pend_stats[0]
                pend_stats[0] = stats
                if prev is not None:
                    prev()

            def flush_stats():
                if pend_stats[0] is not None:
                    pend_stats[0]()
                    pend_stats[0] = None

            def ln_stats():
                ts("dve", MU.v(), pmean, 1.0 / DC, ALU.mult)
                tt("dve", MSQ.v(), MU.v(), MU.v(), ALU.mult)
                stt("dve", MSQ.v(), psq, 1.0 / DC, MSQ.v(), ALU.mult, ALU.subtract)
                act(MSQ.v(), MSQ.v(), AF.Ln, bias=vcol("eps"))
                act(RSTD.v(), MSQ.v(), AF.Exp, scale=-0.5)

            def norm_task(c):
                eng = "dve"
                tt(eng, C.v(c), C.v(c), MU.v(), ALU.subtract)
                tt(eng, C.v(c), C.v(c), RSTD.v(), ALU.mult)
                act(S.v(c), C.v(c), AF.Silu, bias=vcol("lnb", c), scale=vcol("lnw", c))

            def pw_task(tt_, co):
                t0 = tt_ * 512
                pv = PSB[rr[0] % 6].v()
                rr[0] += 1
                pairs = [(PWc[ci].v((co * 128, (co + 1) * 128)).ap, S.v(ci).ap) for ci in range(8)]
                mm_group(pv, pairs, reads=[PWc[ci].v((co * 128, (co + 1) * 128)) for ci in range(8)] + [S.v()])
                ycv = YC.v(co, (t0, t0 + 512))
                stt("dve", ycv, pv, vcol("pwb", co), ycv, ALU.add, ALU.mult)

            for c in range(8):
                conv_task(0, c)
            dg_build(1, 0)
            dg_build(1, 1)
            flush_stats()
            dump("c", C.v(), (128, NCH, 512))
            ln_stats()
            for c in range(8):
                norm_task(c)
                if c == 7:
                    dump("s", S.v(), (128, NCH, 512))
                conv_task(1, c)
                dg_build(1, c + 2)
            flush_stats()
            for co in range(8):
                pw_task(0, co)
            ln_stats()

            dma("sp", NPOST.v(), npost_d, "c_npost")
            op_cnt = [0]

            def outproj_task(i):
                tok = i * 128
                xe, ot = XE[0], OUT[0]
                dma("sp", xe.v(), xo_d[tok:tok + 128, :], "xe0")
                base = (op_cnt[0] % 2) * 4
                op_cnt[0] += 1
                for dsl in range(4):
                    pv = PSB[base + dsl].v()
                    pairs = []
                    rd = []
                    for cc in range(16):
                        ysrc = YL if cc < 8 else YC
                        yv = ysrc.v(cc % 8, (tok, tok + 128))
                        wv = WOUT[cc].v((dsl * 512, (dsl + 1) * 512))
                        pairs.append((yv.ap, wv.ap))
                        rd += [yv, wv]
                    mm_group(pv, pairs, reads=rd)
                sb_ = VC["ssq"] + 32 + (i % 2) * 8
                sqc = vecs.v((sb_, sb_ + 4))
                for dsl in range(4):
                    act(junk.v((0, 512)), PSB[base + dsl].v(), AF.Square, accum=vecs.v((sb_ + dsl, sb_ + dsl + 1)))
                tot = vecs.v((sb_ + 4, sb_ + 5))
                sd = vecs.v((sb_ + 5, sb_ + 6))
                rs = vecs.v((sb_ + 6, sb_ + 7))
                P.op("dve", lambda e: e.tensor_reduce(out=tot.ap, in_=sqc.ap, axis=mybir.AxisListType.X, op=ALU.add),
                     reads=[sqc], writes=[tot])
                ts("dve", sd, tot, 1.0 / D, ALU.mult, EPS, ALU.add)
                act(sd, sd, AF.Sqrt)
                P.op("dve", lambda e: e.reciprocal(out=rs.ap, in_=sd.ap), reads=[sd], writes=[rs])
                for dsl in range(4):
                    sl = (dsl * 512, (dsl + 1) * 512)
                    stt("dve", ot.v(sl), PSB[base + dsl].v(), rs, NPOST.v(sl), ALU.mult, ALU.mult)
                    if dsl == 1:
                        h0 = (0, 1024)
                        tt("pool", ot.v(h0), ot.v(h0), xe.v(h0), ALU.add)
                        P.op("sp", lambda e: e.dma_start(out=out_d[tok:tok + 128, 0:1024], in_=ot.v(h0).ap),
                             reads=[ot.v(h0)], dma_key="out0a")
                h1 = (1024, 2048)
                tt("dve", ot.v(h1), ot.v(h1), xe.v(h1), ALU.add)
                P.op("sp", lambda e: e.dma_start(out=out_d[tok:tok + 128, 1024:2048], in_=ot.v(h1).ap),
                     reads=[ot.v(h1)], dma_key="out0b")

            for c in range(8):
                norm_task(c)
            for i in range(4):
                outproj_task(i)
            for co in range(8):
                pw_task(1, co)
            dump("yc", YC.v(), (128, NCH, TO))
            chk("p2")
            for i in range(4, 8):
                outproj_task(i)
        except _Stop:
            pass

        P.emit(stack)
    return nc, dbg_d


_CACHE = {}


def _layout(inp):
    f = np.float32
    x = np.asarray(inp["x"], f)
    meta = np.asarray(inp["meta_tokens"], f)
    w_in = np.asarray(inp["w_in"], f)[0]
    w_in_r = np.ascontiguousarray(w_in.reshape(16, 128, 40, 128).transpose(2, 1, 0, 3))

    def cols(v, n):
        return np.asarray(v, f).reshape(n, 128).T

    def bd(w):
        w = np.asarray(w, f)[0]
        o = np.zeros((128, 8, 128), f)
        for j in range(8):
            for a in range(2):
                o[64 * a:64 * a + 64, j, 64 * a:64 * a + 64] = w[2 * j + a]
        return o
    vec = np.zeros((128, 512), f)
    c = 0
    parts = [cols(inp["b_in"][0], 40), cols(inp["lru_conv_b"][0], 8), cols(inp["b_gate_a"][0], 8),
             cols(inp["b_gate_x"][0], 8), cols(inp["lru_lambda"][0], 8), cols(inp["conf_dw_b"][0], 8),
             cols(inp["conf_ln_w"][0], 8), cols(inp["conf_ln_b"][0], 8), cols(inp["conf_pw_b"][0], 8),
             np.asarray(inp["lru_conv_w"], f)[0].reshape(4, 8, 128).transpose(2, 1, 0).reshape(128, 32),
             np.asarray(inp["conf_dw_w"], f)[0].reshape(31, 8, 128).transpose(2, 1, 0).reshape(128, 248)]
    for p_ in parts:
        vec[:, c:c + p_.shape[1]] = p_
        c += p_.shape[1]
    shared = {
        "w_in_r": w_in_r,
        "w_out": np.ascontiguousarray(np.asarray(inp["w_out"], f)[0]),
        "pw_w": np.ascontiguousarray(np.asarray(inp["conf_pw_w"], f)[0]),
        "wga_bd": bd(inp["w_gate_a"]), "wgx_bd": bd(inp["w_gate_x"]),
        "pre_bc": np.ascontiguousarray(np.broadcast_to(np.asarray(inp["pre_norm_w"], f)[0][None, :], (128, D))),
        "post_bc": np.ascontiguousarray(np.broadcast_to(np.asarray(inp["post_norm_w"], f)[0][None, :], (128, D))),
        "vecs": vec,
    }
    maps = []
    for core in range(8):
        b, h = core // 2, core % 2
        xp = np.zeros((9 * 128, D), f)
        if h == 0:
            xp[1024:1040] = meta
        else:
            xp[0:16] = meta
            xp[16:1040] = x[b, 0:1024]
        m = dict(shared)
        m["xp"] = xp
        m["xo"] = np.ascontiguousarray(x[b, h * 1024:(h + 1) * 1024])
        m["flag"] = np.full((128, 1), float(h), f)
        maps.append(m)
    return maps


def kernel(**inputs):
    if "nc" not in _CACHE:
        _CACHE["nc"] = build_program(False)[0]
    nc = _CACHE["nc"]
    maps = _layout(inputs)
    res = run_bass_kernel_spmd(nc, maps, core_ids=list(range(8)))
    out = np.zeros((4, 2048, D), np.float32)
    for core in range(8):
        b, h = core // 2, core % 2
        out[b, h * 1024:(h + 1) * 1024] = res.results[core]["out"]
    return out
```
